# Optimizing a Trainium2 kernel written in Bass

```python
import jax
import jax.numpy as jnp
from jax import lax
import numpy as np

D_MODEL = 2048
BATCH = 2
SEQ = 4096
DEPTH = 4
DEC_BATCH = 8
DEC_SEQ = 8
PAST_LEN = 16384
PAGE_SIZE = 128

N_HEADS = 16
HEAD_DIM = 128
N_KV_HEADS = 4
GROUP = N_HEADS // N_KV_HEADS
CMP_BLOCK = 32
CMP_STRIDE = 16
SEL_BLOCK = 64
TOP_N = 16
N_LOCAL_FORCED = 2
WINDOW = 512
Q_BLOCK = 64
SCALE = HEAD_DIM ** -0.5
ATTN_IN = N_HEADS * HEAD_DIM + 3 * 2 * N_KV_HEADS * HEAD_DIM + 3 * N_HEADS
CONV_W = 3
D_CONV = D_MODEL
D_FF = ((8 * D_MODEL // 3 + 255) // 256) * 256
N_ATTN_LAYERS = (DEPTH + 1) // 2
N_CONV_LAYERS = DEPTH // 2
EPS = 1e-6

kernel_name = 'nsa_shortconv_hybrid_step'


def rmsnorm(x, g):
    xf = x.astype(jnp.float32)
    y = xf * lax.rsqrt(jnp.mean(xf * xf, axis=-1, keepdims=True) + EPS)
    return (y * g.astype(jnp.float32)).astype(x.dtype)


def masked_softmax(s, mask):
    s = jnp.where(mask, s.astype(jnp.float32), -jnp.inf)
    m = jnp.max(s, axis=-1, keepdims=True)
    e = jnp.exp(s - jnp.where(jnp.isfinite(m), m, 0.0))
    return e / jnp.maximum(jnp.sum(e, axis=-1, keepdims=True), 1e-30)


def nsa_project(h, w_in):
    bsz, t = h.shape[:2]
    p = jnp.einsum('btd,de->bte', h, w_in)
    n_q = N_HEADS * HEAD_DIM
    n_kv = 3 * 2 * N_KV_HEADS * HEAD_DIM
    q = p[..., :n_q].reshape(bsz, t, N_KV_HEADS, GROUP, HEAD_DIM).transpose(0, 2, 3, 1, 4)
    kv = p[..., n_q:n_q + n_kv].reshape(bsz, t, 3, N_KV_HEADS, 2, HEAD_DIM)
    gates = jax.nn.sigmoid(p[..., n_q + n_kv:].astype(jnp.float32)).astype(p.dtype)
    gates = gates.reshape(bsz, t, 3, N_KV_HEADS, GROUP)
    return q, kv[:, :, 0], kv[:, :, 1], kv[:, :, 2], gates


def compress(kv, pool_w, pe, w1, w2):
    bsz, n_rows = kv.shape[:2]
    n_half = CMP_BLOCK // CMP_STRIDE
    n_chunks = n_rows // CMP_STRIDE
    n_cmp = n_chunks - n_half + 1
    chunks = kv[:, :n_chunks * CMP_STRIDE].reshape(bsz, n_chunks, CMP_STRIDE, N_KV_HEADS, 2, HEAD_DIM)
    w_half = pool_w.reshape(2, n_half, CMP_STRIDE)
    part = jnp.einsum('bcjgkd,khj->hbcgkd', chunks, w_half)
    pooled = part[0, :, 0:n_cmp]
    for hf in range(1, n_half):
        pooled = pooled + part[hf, :, hf:hf + n_cmp]
    pooled = pooled + jnp.einsum('kj,kjd->kd', pool_w, pe)
    hid = jax.nn.silu(jnp.einsum('bcgkd,kde->bcgke', pooled, w1))
    return jnp.einsum('bcgke,ked->bcgkd', hid, w2)


def cmp_attend(q, ckv, q_pos):
    n_cmp = ckv.shape[1]
    s = jnp.einsum('bgrtd,bcgd->bgrtc', q, ckv[..., 0, :]) * SCALE
    last = jnp.arange(n_cmp) * CMP_STRIDE + CMP_BLOCK - 1
    p = masked_softmax(s, last[None, :] <= q_pos[:, None])
    o = jnp.einsum('bgrtc,bcgd->bgrtd', p.astype(q.dtype), ckv[..., 1, :])
    return o, p


def select_blocks(p_cmp, q_pos, n_sel):
    n_cmp = p_cmp.shape[-1]
    c_start = jnp.arange(n_cmp) * CMP_STRIDE
    b_start = jnp.arange(n_sel) * SEL_BLOCK
    overlap = (c_start[:, None] < b_start[None, :] + SEL_BLOCK) & (c_start[:, None] + CMP_BLOCK > b_start[None, :])
    imp = jnp.einsum('bgrtc,cj->bgtj', p_cmp, overlap.astype(p_cmp.dtype))
    blk = jnp.arange(n_sel)[None, :]
    cur = (q_pos // SEL_BLOCK)[:, None]
    valid = b_start[None, :] <= q_pos[:, None]
    forced = (blk == 0) | ((blk <= cur) & (blk > cur - N_LOCAL_FORCED))
    score = jnp.where(forced, jnp.inf, jnp.where(valid, imp, -jnp.inf))
    top_score, idx = lax.top_k(score, min(TOP_N, n_sel))
    return idx, top_score > -jnp.inf


def sel_attend(q, kv_g, blk_idx, blk_ok, q_pos):
    bsz, g, t, n = blk_idx.shape
    k_pos = blk_idx[..., None] * SEL_BLOCK + jnp.arange(SEL_BLOCK)
    mask = (blk_ok[..., None] & (k_pos <= q_pos[:, None, None])).reshape(bsz, g, 1, t, n * SEL_BLOCK)
    k = kv_g[..., 0, :].reshape(bsz, g, t, n * SEL_BLOCK, HEAD_DIM)
    v = kv_g[..., 1, :].reshape(bsz, g, t, n * SEL_BLOCK, HEAD_DIM)
    s = jnp.einsum('bgrtd,bgtkd->bgrtk', q, k) * SCALE
    p = masked_softmax(s, mask)
    return jnp.einsum('bgrtk,bgtkd->bgrtd', p.astype(q.dtype), v)


def win_attend(q, kv, k_pos, q_pos):
    s = jnp.einsum('bgrtd,blgd->bgrtl', q, kv[..., 0, :]) * SCALE
    rel = q_pos[:, None] - k_pos[None, :]
    mask = (rel >= 0) & (rel < WINDOW) & (k_pos[None, :] >= 0)
    p = masked_softmax(s, mask)
    return jnp.einsum('bgrtl,blgd->bgrtd', p.astype(q.dtype), kv[..., 1, :])


def nsa_merge(o, gates, w_out):
    mixed = jnp.einsum('cbgrtd,btcgr->btgrd', o, gates)
    return jnp.einsum('btgrd,grde->bte', mixed, w_out.reshape(N_KV_HEADS, GROUP, HEAD_DIM, -1))


def nsa_prompt(h, w_in, pool_w, pe, w1, w2, w_out):
    bsz, t, _ = h.shape
    q, kv_cmp, kv_sel, kv_win, gates = nsa_project(h, w_in)
    ckv = compress(kv_cmp, pool_w, pe, w1, w2)
    n_sel = t // SEL_BLOCK
    sel_blocks = kv_sel.reshape(bsz, n_sel, SEL_BLOCK, N_KV_HEADS, 2, HEAD_DIM).transpose(0, 3, 1, 2, 4, 5)
    win_pad = jnp.pad(kv_win, ((0, 0), (WINDOW, 0), (0, 0), (0, 0), (0, 0)))
    n_qb = t // Q_BLOCK
    q_blocks = q.reshape(bsz, N_KV_HEADS, GROUP, n_qb, Q_BLOCK, HEAD_DIM).transpose(3, 0, 1, 2, 4, 5)
    b_idx = jnp.arange(bsz)[:, None, None, None]
    g_idx = jnp.arange(N_KV_HEADS)[None, :, None, None]

    def one_block(args):
        c, q_c = args
        start = c * Q_BLOCK
        q_pos = start + jnp.arange(Q_BLOCK)
        o_c, p_c = cmp_attend(q_c, ckv, q_pos)
        idx, ok = select_blocks(p_c, q_pos, n_sel)
        o_s = sel_attend(q_c, sel_blocks[b_idx, g_idx, idx], idx, ok, q_pos)
        kv_w = lax.dynamic_slice_in_dim(win_pad, start, WINDOW + Q_BLOCK, axis=1)
        o_w = win_attend(q_c, kv_w, start - WINDOW + jnp.arange(WINDOW + Q_BLOCK), q_pos)
        return jnp.stack([o_c, o_s, o_w], axis=0)

    o = lax.map(one_block, (jnp.arange(n_qb), q_blocks))
    o = o.transpose(1, 2, 3, 4, 0, 5, 6).reshape(3, bsz, N_KV_HEADS, GROUP, t, HEAD_DIM)
    y = nsa_merge(o, gates, w_out)
    return y, kv_cmp, kv_sel, kv_win[:, -min(WINDOW, t):]


def nsa_sample(h, cache_cmp_kv, cache_sel_kv, win_buf, page_table, l, w_in, pool_w, pe, w1, w2, w_out):
    bsz, t, _ = h.shape
    past = page_table.shape[1] * PAGE_SIZE
    q, kv_cmp, kv_sel, kv_win, gates = nsa_project(h, w_in)
    q_pos = past + jnp.arange(t)
    past_cmp = cache_cmp_kv[l, page_table].reshape(bsz, past, N_KV_HEADS, 2, HEAD_DIM)
    ckv = compress(jnp.concatenate([past_cmp, kv_cmp], axis=1), pool_w, pe, w1, w2)
    o_c, p_c = cmp_attend(q, ckv, q_pos)
    n_sel = -(-(past + t) // SEL_BLOCK)
    idx, ok = select_blocks(p_c, q_pos, n_sel)
    b_idx = jnp.arange(bsz)[:, None, None, None]
    g_idx = jnp.arange(N_KV_HEADS)[None, :, None, None]
    n_past_blk = past // SEL_BLOCK
    blk_per_page = PAGE_SIZE // SEL_BLOCK
    past_blk = jnp.minimum(idx, n_past_blk - 1)
    page = page_table[b_idx, past_blk // blk_per_page]
    row = (past_blk % blk_per_page)[..., None] * SEL_BLOCK + jnp.arange(SEL_BLOCK)
    kv_past = cache_sel_kv[l, page[..., None], row, g_idx[..., None]]
    n_new_blk = n_sel - n_past_blk
    new_pad = jnp.pad(kv_sel, ((0, 0), (0, n_new_blk * SEL_BLOCK - t), (0, 0), (0, 0), (0, 0)))
    new_blocks = new_pad.reshape(bsz, n_new_blk, SEL_BLOCK, N_KV_HEADS, 2, HEAD_DIM).transpose(0, 3, 1, 2, 4, 5)
    kv_new = new_blocks[b_idx, g_idx, jnp.clip(idx - n_past_blk, 0, n_new_blk - 1)]
    kv_g = jnp.where((idx >= n_past_blk)[..., None, None, None], kv_new, kv_past)
    o_s = sel_attend(q, kv_g, idx, ok, q_pos)
    n_buf = win_buf.shape[1]
    kv_w = jnp.concatenate([win_buf, kv_win], axis=1)
    o_w = win_attend(q, kv_w, past - n_buf + jnp.arange(n_buf + t), q_pos)
    y = nsa_merge(jnp.stack([o_c, o_s, o_w], axis=0), gates, w_out)
    return y, kv_cmp, kv_sel, kv_w[:, -n_buf:]


def shortconv(h, buf, w_in, conv_k, w_out):
    t = h.shape[1]
    p = jnp.einsum('btd,de->bte', h, w_in)
    b_gate, c_gate, x_in = jnp.split(p, 3, axis=-1)
    u_ext = jnp.concatenate([buf, c_gate * x_in], axis=1)
    v = conv_k[0] * u_ext[:, 0:t]
    for j in range(1, CONV_W):
        v = v + conv_k[j] * u_ext[:, j:j + t]
    y = jnp.einsum('bte,ed->btd', b_gate * v, w_out)
    return y, u_ext[:, -(CONV_W - 1):]


def swiglu(h, w_in, w_out):
    g, u = jnp.split(jnp.einsum('btd,df->btf', h, w_in), 2, axis=-1)
    return jnp.einsum('btf,fd->btd', jax.nn.silu(g) * u, w_out)


def setup_inputs(seed: int = 0) -> dict:
    key = jax.random.key(seed)
    ks = jax.random.split(key, 32)
    f32 = jnp.float32
    n_pages = PAST_LEN // PAGE_SIZE
    n_pool = (DEC_BATCH * n_pages * 5) // 4

    def nrm(k, shape, scale):
        return jax.random.normal(k, shape, f32) * scale

    def gain(k, shape):
        return 1.0 + 0.02 * jax.random.normal(k, shape, f32)

    perm = jax.random.permutation(ks[6], n_pool)
    page_table = perm[:DEC_BATCH * n_pages].reshape(DEC_BATCH, n_pages).astype(jnp.int32)
    kv_row = (N_KV_HEADS, 2, HEAD_DIM)
    return {
        'x_prompt': nrm(ks[0], (BATCH, SEQ, D_MODEL), 1.0),
        'x_sample': nrm(ks[1], (DEC_BATCH, DEC_SEQ, D_MODEL), 1.0),
        'cache_cmp_kv': nrm(ks[2], (N_ATTN_LAYERS, n_pool, PAGE_SIZE) + kv_row, 1.0),
        'cache_sel_kv': nrm(ks[3], (N_ATTN_LAYERS, n_pool, PAGE_SIZE) + kv_row, 1.0),
        'state_win_kv': nrm(ks[4], (N_ATTN_LAYERS, DEC_BATCH, min(WINDOW, PAST_LEN)) + kv_row, 1.0),
        'state_conv': nrm(ks[5], (N_CONV_LAYERS, DEC_BATCH, CONV_W - 1, D_CONV), 1.0),
        'page_table': page_table,
        'attn_norm': gain(ks[7], (N_ATTN_LAYERS, D_MODEL)),
        'attn_w_in': nrm(ks[8], (N_ATTN_LAYERS, D_MODEL, ATTN_IN), D_MODEL ** -0.5),
        'attn_cmp_pool': CMP_BLOCK ** -0.5 * (1.0 + 0.1 * jax.random.normal(ks[9], (N_ATTN_LAYERS, 2, CMP_BLOCK), f32)),
        'attn_cmp_pe': nrm(ks[10], (N_ATTN_LAYERS, 2, CMP_BLOCK, HEAD_DIM), 0.5),
        'attn_cmp_w1': nrm(ks[11], (N_ATTN_LAYERS, 2, HEAD_DIM, HEAD_DIM), HEAD_DIM ** -0.5),
        'attn_cmp_w2': nrm(ks[12], (N_ATTN_LAYERS, 2, HEAD_DIM, HEAD_DIM), HEAD_DIM ** -0.5),
        'attn_w_out': nrm(ks[13], (N_ATTN_LAYERS, N_HEADS * HEAD_DIM, D_MODEL), (N_HEADS * HEAD_DIM) ** -0.5),
        'conv_norm': gain(ks[14], (N_CONV_LAYERS, D_MODEL)),
        'conv_w_in': nrm(ks[15], (N_CONV_LAYERS, D_MODEL, 3 * D_CONV), D_MODEL ** -0.5),
        'conv_kernel': nrm(ks[16], (N_CONV_LAYERS, CONV_W, D_CONV), CONV_W ** -0.5),
        'conv_w_out': nrm(ks[17], (N_CONV_LAYERS, D_CONV, D_MODEL), D_CONV ** -0.5),
        'ffn_norm': gain(ks[18], (DEPTH, D_MODEL)),
        'ffn_w_in': nrm(ks[19], (DEPTH, D_MODEL, 2 * D_FF), D_MODEL ** -0.5),
        'ffn_w_out': nrm(ks[20], (DEPTH, D_FF, D_MODEL), D_FF ** -0.5),
        'final_norm': gain(ks[21], (D_MODEL,)),
    }


def reference(x_prompt, x_sample, cache_cmp_kv, cache_sel_kv, state_win_kv, state_conv, page_table,
              attn_norm, attn_w_in, attn_cmp_pool, attn_cmp_pe, attn_cmp_w1, attn_cmp_w2, attn_w_out,
              conv_norm, conv_w_in, conv_kernel, conv_w_out, ffn_norm, ffn_w_in, ffn_w_out, final_norm):
    xp, xs = x_prompt, x_sample
    cmp_p, sel_p, win_p, conv_p = [], [], [], []
    cmp_s, sel_s, win_s, conv_s = [], [], [], []
    for i in range(DEPTH):
        l = i // 2
        if i % 2 == 0:
            y_p, c_new, s_new, w_new = nsa_prompt(
                rmsnorm(xp, attn_norm[l]), attn_w_in[l], attn_cmp_pool[l], attn_cmp_pe[l],
                attn_cmp_w1[l], attn_cmp_w2[l], attn_w_out[l])
            cmp_p.append(c_new)
            sel_p.append(s_new)
            win_p.append(w_new)
            y_s, c_new, s_new, w_new = nsa_sample(
                rmsnorm(xs, attn_norm[l]), cache_cmp_kv, cache_sel_kv, state_win_kv[l], page_table, l,
                attn_w_in[l], attn_cmp_pool[l], attn_cmp_pe[l], attn_cmp_w1[l], attn_cmp_w2[l], attn_w_out[l])
            cmp_s.append(c_new)
            sel_s.append(s_new)
            win_s.append(w_new)
        else:
            zero_buf = jnp.zeros((xp.shape[0], CONV_W - 1, D_CONV), xp.dtype)
            y_p, b_new = shortconv(rmsnorm(xp, conv_norm[l]), zero_buf, conv_w_in[l], conv_kernel[l], conv_w_out[l])
            conv_p.append(b_new)
            y_s, b_new = shortconv(rmsnorm(xs, conv_norm[l]), state_conv[l], conv_w_in[l], conv_kernel[l], conv_w_out[l])
            conv_s.append(b_new)
        xp = xp + y_p
        xs = xs + y_s
        xp = xp + swiglu(rmsnorm(xp, ffn_norm[i]), ffn_w_in[i], ffn_w_out[i])
        xs = xs + swiglu(rmsnorm(xs, ffn_norm[i]), ffn_w_in[i], ffn_w_out[i])
    return (rmsnorm(xp, final_norm), rmsnorm(xs, final_norm),
            jnp.stack(cmp_p), jnp.stack(sel_p), jnp.stack(win_p), jnp.stack(conv_p),
            jnp.stack(cmp_s), jnp.stack(sel_s), jnp.stack(win_s), jnp.stack(conv_s))
```

```python
from contextlib import ExitStack
import numpy as np
import ml_dtypes
import concourse.bass as bass
import concourse.mybir as mybir
from concourse.bass_utils import run_bass_kernel_spmd

F32 = mybir.dt.float32
BF16 = mybir.dt.bfloat16
I32 = mybir.dt.int32
AF = mybir.ActivationFunctionType
ALU = mybir.AluOpType
AX = mybir.AxisListType

D = 2048
KC = 16
NTP = 1024
NS = 8
NT = NTP + NS
TG = [(0, 512), (512, 512), (1024, 8)]
DFF = 5632
ATTN_IN = 5168
SCALE = 128 ** -0.5
EPS = 1e-6
BIG = 30000.0
NPAGE = 128
SEM_LIMIT = 12000
XR = 4096


class Tracker:
    def __init__(self, nc):
        self.nc = nc
        self.engs = {"pe": nc.tensor, "act": nc.scalar, "dve": nc.vector, "pool": nc.gpsimd, "sp": nc.sync}
        self.cnt = {e: 0 for e in self.engs}
        self.gen = {e: 0 for e in self.engs}
        self.semobj = {}
        for e in self.engs:
            self.semobj[("e", e, 0)] = nc.alloc_semaphore(f"s_{e}_0")
        self.waited = {}
        self.res = {}
        self.nslots = {"sp": 8, "pool": 6, "act": 2}
        self.slot_next = {q: 0 for q in self.nslots}
        self.slot_cnt = {}
        for q, n in self.nslots.items():
            for s in range(n):
                self.semobj[("d", q, s)] = nc.alloc_semaphore(f"d_{q}_{s}")
                self.slot_cnt[(q, s)] = 0
        self.semobj[("cc",)] = nc.alloc_semaphore("ccsem")
        self.cc_cnt = 0

    def _wait(self, eng, tok):
        sk, val = tok
        if val <= 0:
            return
        if eng == "pe" and sk[0] == "e" and sk[1] == "pe":
            return
        if self.waited.get((eng, sk), 0) >= val:
            return
        self.engs[eng].wait_ge(self.semobj[sk], val)
        self.waited[(eng, sk)] = val

    def _deps(self, reads, writes):
        deps = []
        for k in reads:
            r = self.res.get(k)
            if r and r["w"]:
                deps.append(r["w"])
        for k in writes:
            r = self.res.get(k)
            if r:
                if r["w"]:
                    deps.append(r["w"])
                deps.extend(r["r"])
        return deps

    def _record(self, tok, reads, writes):
        for k in reads:
            r = self.res.setdefault(k, {"w": None, "r": []})
            r["r"].append(tok)
            if len(r["r"]) > 48:
                best = {}
                for (sk, v) in r["r"]:
                    best[sk] = max(best.get(sk, 0), v)
                r["r"] = list(best.items())
        for k in writes:
            self.res[k] = {"w": tok, "r": []}

    @staticmethod
    def _is_psum(k):
        return k == "psb" or (isinstance(k, tuple) and k[0] == "ps")

    def op(self, eng, fn, reads=(), writes=()):
        pr = [k for k in reads if self._is_psum(k)]
        if pr:
            reads = [k for k in reads if not self._is_psum(k)]
            writes = list(writes) + pr
        for tok in self._deps(reads, writes):
            self._wait(eng, tok)
        if self.cnt[eng] >= SEM_LIMIT:
            self.gen[eng] += 1
            self.cnt[eng] = 0
            self.semobj[("e", eng, self.gen[eng])] = self.nc.alloc_semaphore(f"s_{eng}_{self.gen[eng]}")
        sk = ("e", eng, self.gen[eng])
        ins = fn(self.engs[eng])
        self.cnt[eng] += 1
        ins.then_inc(self.semobj[sk], 1)
        tok = (sk, self.cnt[eng])
        self._record(tok, reads, writes)
        return tok

    def _dma_common(self, q, reads, writes):
        for tok in self._deps(reads, writes):
            self._wait(q, tok)
        s = self.slot_next[q]
        self.slot_next[q] = (s + 1) % self.nslots[q]
        sk = ("d", q, s)
        self._wait(q, (sk, 16 * self.slot_cnt[(q, s)]))
        return s, sk

    def dma(self, q, out, in_, reads=(), writes=(), **kw):
        s, sk = self._dma_common(q, reads, writes)
        ins = self.engs[q].dma_start(out=out, in_=in_, **kw)
        self.slot_cnt[(q, s)] += 1
        ins.then_inc(self.semobj[sk], 16)
        tok = (sk, 16 * self.slot_cnt[(q, s)])
        self._record(tok, reads, writes)
        return tok

    def gather(self, out, in_, idx_ap, reads=(), writes=()):
        q = "pool"
        s, sk = self._dma_common(q, reads, writes)
        ins = self.nc.gpsimd.indirect_dma_start(
            out=out, out_offset=None, in_=in_,
            in_offset=bass.IndirectOffsetOnAxis(ap=idx_ap, axis=0))
        self.slot_cnt[(q, s)] += 1
        ins.then_inc(self.semobj[sk], 16)
        tok = (sk, 16 * self.slot_cnt[(q, s)])
        self._record(tok, reads, writes)
        return tok

    def collective(self, kind, groups, in_ap, out_ap, reads=(), writes=()):
        for tok in self._deps(reads, writes):
            self._wait("pool", tok)
        ins = self.nc.gpsimd.collective_compute(kind, ALU.bypass, replica_groups=groups,
                                                ins=[in_ap], outs=[out_ap])
        self.cc_cnt += 1
        ins.then_inc(self.semobj[("cc",)])
        tok = (("cc",), self.cc_cnt)
        self._record(tok, reads, writes)
        return tok

    def all_tokens(self):
        toks = []
        for e in self.engs:
            toks.append((("e", e, self.gen[e]), self.cnt[e]))
        for (q, s), n in self.slot_cnt.items():
            toks.append((("d", q, s), 16 * n))
        toks.append((("cc",), self.cc_cnt))
        return toks

    def barrier(self):
        toks = self.all_tokens()
        for e in self.engs:
            for tok in toks:
                if tok[0][0] == "e" and tok[0][1] == e:
                    continue
                self._wait(e, tok)
        self.res = {}

    def finish(self):
        for tok in self.all_tokens():
            self._wait("sp", tok)


def build(enable_sample=True):
    nc = bass.Bass("TRN2", target_bir_lowering=False)
    T = Tracker(nc)

    def din(name, shape, dt=F32):
        return nc.dram_tensor(name, list(shape), dt, kind="ExternalInput").ap()

    def dout(name, shape, dt=F32):
        return nc.dram_tensor(name, list(shape), dt, kind="ExternalOutput").ap()

    def dint(name, shape, dt=F32):
        return nc.dram_tensor(name, list(shape), dt).ap()

    def sb(name, shape, dt=F32):
        return nc.alloc_sbuf_tensor(name, list(shape), dt)

    uid = {"n": 0}

    def tmp(es, name, shape, dt=F32):
        uid["n"] += 1
        return es.enter_context(nc.sbuf_tensor(f"{name}_{uid['n']}", list(shape), dt))

    xp = din("xp", [NTP, D])
    xs = din("xs", [NS, D])
    gains = din("gains", [9, D])
    attn_w_in = din("attn_w_in", [2, D, ATTN_IN])
    attn_w_out = din("attn_w_out", [2, D, D])
    cmp_pool = din("attn_cmp_pool", [2, 2, 32])
    cmp_pe = din("attn_cmp_pe", [2, 2, 32, 128])
    cmp_w1 = din("attn_cmp_w1", [2, 2, 128, 128])
    cmp_w2 = din("attn_cmp_w2", [2, 2, 128, 128])
    conv_w_in = din("conv_w_in", [2, D, 3 * D])
    conv_kernel = din("conv_kernel", [2, 3, D])
    conv_w_out = din("conv_w_out", [2, D, D])
    ffn_w_in = din("ffn_w_in", [4, D, 2 * DFF])
    ffn_w_out = din("ffn_w_out", [4, DFF, D])
    state_conv = din("state_conv", [2, 2, D])
    c_identf = din("c_identf", [128, 128])
    c_identb = din("c_identb", [128, 128], BF16)
    c_i4 = din("c_i4", [128, 512], BF16)
    c_pmmask = din("c_pmmask", [128, 16])
    c_oh16 = din("c_oh16", [128, 16])
    c_cmpmask = din("c_cmpmask", [8, 128, 2, 128], BF16)
    c_diagmask = din("c_diagmask", [128, 4, 128], BF16)
    c_winmask = din("c_winmask", [128, 8, 128], BF16)
    c_vm = din("c_vm", [8, 128, 64])
    c_fb = din("c_fb", [8, 128, 64])
    c_ov = din("c_ov", [128, 2, 65])
    c_hsel = din("c_hsel", [128, 5])
    state_win = din("state_win", [2, 512, 1024])
    if enable_sample:
        cache_cmp = din("cache_cmp_kv", [2 * 1280 * 128, 1024])
        cache_sel = din("cache_sel_kv", [2 * 1280 * 128, 1024])
        page_tab = din("page_tab", [1, NPAGE], I32)
        c_piota = din("c_piota", [128, 1])
        c_ovs = din("c_ovs", [128, 34])
        c_vms = din("c_vms", [8, 257])
        c_fbs = din("c_fbs", [8, 257])
        c_d8 = din("c_d8", [8, 32], BF16)
        c_tri8 = din("c_tri8", [8, 32], BF16)
        c_wins0 = din("c_wins0", [128, 32], BF16)

    yp = dout("yp", [NTP, D])
    ys = dout("ys", [NS, D])
    kvp = dout("kvp", [2, 3, NTP, 1024])
    kvs = dout("kvs", [2, 3, NS, 1024])
    wins = dout("wins", [2, 512, 1024])
    convp = dout("convp", [2, 2, D])
    convs = dout("convs", [2, 2, D])

    xgk_in = [dint(f"xgk_in{i}", [1024, 512], BF16) for i in range(2)]
    xgk_out = [dint(f"xgk_out{i}", [4096, 512], BF16) for i in range(2)]
    xgv_in = [dint(f"xgv_in{i}", [1024, 512], BF16) for i in range(2)]
    xgv_out = [dint(f"xgv_out{i}", [4096, 512], BF16) for i in range(2)]
    pg_in = dint("pg_in", [128, 1024])
    pg_out = dint("pg_out", [512, 1024])
    hg_in = dint("hg_in", [128, 256])
    hg_out = dint("hg_out", [512, 256])
    GROUPS = [[0, 1, 2, 3], [4, 5, 6, 7]]

    xT = sb("xT", [128, KC, NT])
    wbufs = [sb(f"wb{i}", [128, KC, 512], BF16) for i in range(2)]
    identf = sb("identf", [128, 128])
    identb = sb("identb", [128, 128], BF16)
    onesb = sb("onesb", [128, 128], BF16)
    gn = sb("gn", [128, 9, KC])
    ps = [nc.alloc_psum_tensor(f"ps{i}", [128, 512], F32) for i in range(7)]
    psb = nc.alloc_psum_tensor("psb", [128, 1024], BF16)

    wstate = {"i": 0}

    def wload(src_ap, view):
        i = wstate["i"] % 2
        wstate["i"] += 1
        wb = wbufs[i]
        T.dma("pool", view(wb), src_ap, writes=[("wb", i)])
        return wb, ("wb", i)

    def wload_k16(W2d, c0, ncols):
        src = W2d[:, c0:c0 + ncols].rearrange("(kc p) n -> p kc n", p=128)
        return wload(src, lambda wb: wb[:, :, 0:ncols])

    T.dma("sp", identf[:], c_identf[:, :], writes=["identf"])
    T.dma("sp", identb[:], c_identb[:, :], writes=["identb"])
    T.op("dve", lambda e: e.memset(onesb[:], 1.0), writes=["onesb"])
    with nc.allow_non_contiguous_dma(reason="small gain vectors"):
        T.dma("sp", gn[:], gains.rearrange("g (kc p) -> p g kc", p=128), writes=["gn"])

    HT_KEYS = [("hT", kc) for kc in range(KC)]
    XT_KEYS = [("xT", kc) for kc in range(KC)]

    def load_x():
        with ExitStack() as es:
            xst = [tmp(es, "xst", [128, D]) for _ in range(2)]
            for t in range(9):
                st = xst[t % 2]
                n = 128 if t < 8 else NS
                src = xp[t * 128:(t + 1) * 128, :] if t < 8 else xs[:, :]
                T.dma("sp", st[0:n, :], src, writes=[("xst", t % 2)])
                for q in range(4):
                    bi = q % 2
                    for k in range(4):
                        kc = q * 4 + k
                        T.op("pe", lambda e, kc=kc, k=k, bi=bi, n=n, st=st: e.transpose(
                            out=ps[bi][:, k * 128:k * 128 + n], in_=st[0:n, kc * 128:(kc + 1) * 128],
                            identity=identf[0:n, 0:n]),
                            reads=[("xst", t % 2), "identf"], writes=[("ps", bi)])
                    T.op("dve", lambda e, q=q, bi=bi, n=n, t=t: e.tensor_copy(
                        out=xT[:, q * 4:q * 4 + 4, t * 128:t * 128 + n],
                        in_=ps[bi][:].rearrange("p (a b) -> p a b", a=4)[:, :, 0:n]),
                        reads=[("ps", bi)], writes=[("xT", kc2) for kc2 in range(q * 4, q * 4 + 4)])
            T.barrier()

    def rmsnorm(gi, out, okey="hT"):
        with ExitStack() as esn:
            rstd = tmp(esn, "rstd", [128, NT])
            sq = tmp(esn, "sq", [128, 2, NT], BF16)
            _rmsnorm(gi, out, okey, rstd, sq)
            T.barrier()

    def _rmsnorm(gi, out, okey, rstd, sq):
        for kc in range(KC):
            T.op("act", lambda e, kc=kc: e.activation(out=sq[:, kc % 2, :], in_=xT[:, kc, :], func=AF.Square),
                 reads=[("xT", kc)], writes=[("sq", kc % 2)])
            for g_, (t0, tn) in enumerate(TG):
                T.op("pe", lambda e, kc=kc, g_=g_, t0=t0, tn=tn: e.matmul(
                    ps[g_][:, 0:tn], lhsT=onesb[:], rhs=sq[:, kc % 2, t0:t0 + tn],
                    start=(kc == 0), stop=(kc == KC - 1)),
                    reads=[("sq", kc % 2), "onesb"], writes=[("ps", g_)])
        for g_, (t0, tn) in enumerate(TG):
            T.op("dve", lambda e, g_=g_, t0=t0, tn=tn: e.tensor_scalar(
                out=rstd[:, t0:t0 + tn], in0=ps[g_][:, 0:tn], scalar1=1.0 / D, scalar2=EPS,
                op0=ALU.mult, op1=ALU.add), reads=[("ps", g_)], writes=[("rstd", g_)])
            T.op("act", lambda e, t0=t0, tn=tn: e.activation(
                out=rstd[:, t0:t0 + tn], in_=rstd[:, t0:t0 + tn], func=AF.Sqrt),
                reads=[("rstd", g_)], writes=[("rstd", g_)])
            T.op("dve", lambda e, t0=t0, tn=tn: e.reciprocal(
                out=rstd[:, t0:t0 + tn], in_=rstd[:, t0:t0 + tn]),
                reads=[("rstd", g_)], writes=[("rstd", g_)])
        for kc in range(KC):
            T.op("dve", lambda e, kc=kc: e.scalar_tensor_tensor(
                out=out[:, kc, :], in0=xT[:, kc, :], scalar=gn[:, gi, kc:kc + 1], in1=rstd[:],
                op0=ALU.mult, op1=ALU.mult),
                reads=[("xT", kc), "gn", ("rstd", 0), ("rstd", 1), ("rstd", 2)], writes=[(okey, kc)])

    def mm_fm(bank_i, wb, wkey, col0, ncol, rhs_of, nk, t0, tn, rkeys):
        for k in range(nk):
            T.op("pe", lambda e, k=k: e.matmul(ps[bank_i][0:ncol, 0:tn], lhsT=wb[:, k, col0:col0 + ncol],
                                                rhs=rhs_of(k, t0, tn), start=(k == 0), stop=(k == nk - 1)),
                 reads=[wkey] + rkeys, writes=[("ps", bank_i)])

    def add_to_x(bank_i, oc, t0, tn):
        T.op("dve", lambda e: e.tensor_tensor(
            out=xT[:, oc, t0:t0 + tn], in0=xT[:, oc, t0:t0 + tn], in1=ps[bank_i][:, 0:tn], op=ALU.add),
            reads=[("ps", bank_i), ("xT", oc)], writes=[("xT", oc)])

    def out_proj(W2d, rhs_of, rkeys):
        cnt = 0
        for wt in range(4):
            wb, wk = wload_k16(W2d, wt * 512, 512)
            for ct in range(4):
                oc = wt * 4 + ct
                for g_, (t0, tn) in enumerate(TG):
                    bo = 4 + (cnt % 3)
                    cnt += 1
                    mm_fm(bo, wb, wk, ct * 128, 128, rhs_of, KC, t0, tn, rkeys)
                    add_to_x(bo, oc, t0, tn)

    def ffn(li):
        with ExitStack() as es:
            hT = tmp(es, "hT", [128, KC, NT], BF16)
            act = tmp(es, "act", [128, 4, NT], BF16)
            sg = [tmp(es, "sg", [128, 512]) for _ in range(2)]
            wobs = [tmp(es, "wob", [128, 4, 2048], BF16) for _ in range(2)]
            rmsnorm(4 + li, hT)
            Win = ffn_w_in[li]
            Wout = ffn_w_out[li]
            cnt = 0
            hrhs = lambda k, t0, tn: hT[:, k, t0:t0 + tn]
            for blk in range(DFF // 512):
                wg, kg = wload_k16(Win, blk * 512, 512)
                wu, ku = wload_k16(Win, DFF + blk * 512, 512)
                wov = wobs[blk % 2]
                ko = ("wob", blk % 2)
                T.dma("pool", wov[:], Wout[blk * 512:(blk + 1) * 512, :].rearrange("(kc p) n -> p kc n", p=128),
                      writes=[ko])
                for ct in range(4):
                    for g_, (t0, tn) in enumerate(TG):
                        bg, bu = 2 * (cnt % 2), 1 + 2 * (cnt % 2)
                        sgi = cnt % 2
                        cnt += 1
                        mm_fm(bg, wg, kg, ct * 128, 128, hrhs, KC, t0, tn, HT_KEYS)
                        mm_fm(bu, wu, ku, ct * 128, 128, hrhs, KC, t0, tn, HT_KEYS)
                        T.op("act", lambda e, bg=bg, sgi=sgi, tn=tn: e.activation(
                            out=sg[sgi][:, 0:tn], in_=ps[bg][:, 0:tn], func=AF.Silu),
                            reads=[("ps", bg)], writes=[("sg", sgi)])
                        T.op("dve", lambda e, bu=bu, sgi=sgi, ct=ct, t0=t0, tn=tn: e.tensor_tensor(
                            out=act[:, ct, t0:t0 + tn], in0=sg[sgi][:, 0:tn], in1=ps[bu][:, 0:tn], op=ALU.mult),
                            reads=[("sg", sgi), ("ps", bu)], writes=[("act", ct, g_)])
                for oc in range(KC):
                    for g_, (t0, tn) in enumerate(TG):
                        bo = 4 + (cnt % 3)
                        cnt += 1
                        for c4 in range(4):
                            T.op("pe", lambda e, c4=c4, bo=bo, oc=oc, t0=t0, tn=tn: e.matmul(
                                ps[bo][:, 0:tn], lhsT=wov[:, c4, oc * 128:(oc + 1) * 128],
                                rhs=act[:, c4, t0:t0 + tn], start=(c4 == 0), stop=(c4 == 3)),
                                reads=[ko, ("act", c4, g_)], writes=[("ps", bo)])
                        add_to_x(bo, oc, t0, tn)
            T.barrier()

    def conv_layer(lc):
        W = conv_w_in[lc]
        with ExitStack() as es:
            hT = tmp(es, "hT", [128, KC, NT], BF16)
            uT = tmp(es, "uT", [128, KC, NT], BF16)
            uh = tmp(es, "uh", [128, KC, 8, 2])
            uhs = tmp(es, "uhs", [128, KC, 2])
            shs = tmp(es, "shs", [128, KC, 2])
            uhg = tmp(es, "uhg", [128, 4, 256])
            halo = tmp(es, "halo", [128, KC, 8, 2])
            kcol = tmp(es, "kcol", [128, KC, 3])
            hsel = tmp(es, "hsel", [128, 5])
            ctmp = [tmp(es, "ctmp", [128, 512]) for _ in range(2)]
            u32 = [tmp(es, "u32", [128, 512]) for _ in range(2)]
            uext = tmp(es, "uext", [128, 8, 130])
            uexs = tmp(es, "uexs", [128, 10])
            v32 = tmp(es, "v32", [128, NT])
            with nc.allow_non_contiguous_dma(reason="small conv params"):
                for jj in range(3):
                    T.dma("sp", kcol[:, :, jj], conv_kernel[lc, jj].rearrange("(cc p) -> p cc", p=128),
                          writes=[("kcol", jj)])
                for jj in range(2):
                    T.dma("sp", shs[:, :, jj], state_conv[lc, jj].rearrange("(cc p) -> p cc", p=128),
                          writes=[("shs", jj)])
            T.dma("sp", hsel[:], c_hsel[:, :], writes=["hsel"])
            rmsnorm(2 + lc, hT)
            hrhs = lambda k, t0, tn: hT[:, k, t0:t0 + tn]
            cnt = 0
            for wt in range(4):
                wc, kc_ = wload_k16(W, D + wt * 512, 512)
                wx, kx_ = wload_k16(W, 2 * D + wt * 512, 512)
                for ct in range(4):
                    cc = wt * 4 + ct
                    for g_, (t0, tn) in enumerate(TG):
                        bc, bx = 2 * (cnt % 2), 1 + 2 * (cnt % 2)
                        si = cnt % 2
                        cnt += 1
                        mm_fm(bc, wc, kc_, ct * 128, 128, hrhs, KC, t0, tn, HT_KEYS)
                        mm_fm(bx, wx, kx_, ct * 128, 128, hrhs, KC, t0, tn, HT_KEYS)
                        T.op("act", lambda e, bc=bc, si=si, tn=tn: e.copy(out=ctmp[si][:, 0:tn], in_=ps[bc][:, 0:tn]),
                             reads=[("ps", bc)], writes=[("ctmp", si)])
                        T.op("dve", lambda e, bx=bx, si=si, tn=tn: e.tensor_tensor(
                            out=u32[si][:, 0:tn], in0=ctmp[si][:, 0:tn], in1=ps[bx][:, 0:tn], op=ALU.mult),
                            reads=[("ctmp", si), ("ps", bx)], writes=[("u32", si)])
                        T.op("act", lambda e, si=si, cc=cc, t0=t0, tn=tn: e.copy(
                            out=uT[:, cc, t0:t0 + tn], in_=u32[si][:, 0:tn]),
                            reads=[("u32", si)], writes=[("uT", cc, g_)])
                        if g_ < 2:
                            T.op("dve", lambda e, si=si, cc=cc, g_=g_: e.tensor_copy(
                                out=uh[:, cc, 4 * g_:4 * g_ + 4, :],
                                in_=u32[si][:].rearrange("p (a b) -> p a b", a=4)[:, :, 126:128]),
                                reads=[("u32", si)], writes=[("uh", cc, g_)])
                        else:
                            T.op("dve", lambda e, si=si, cc=cc: e.tensor_copy(
                                out=uhs[:, cc, :], in_=u32[si][:, 6:8]),
                                reads=[("u32", si)], writes=[("uhs", cc)])
            uhkeys = [("uh", cc, g_) for cc in range(KC) for g_ in range(2)]
            T.dma("sp", hg_in[:, :], uh[:].rearrange("p a b c -> p (a b c)"), reads=uhkeys, writes=["hg_in"])
            T.collective("AllGather", GROUPS, hg_in.opt(), hg_out.opt(), reads=["hg_in"], writes=["hg_out"])
            T.dma("sp", uhg[:], hg_out.rearrange("(r p) n -> p r n", p=128), reads=["hg_out"], writes=["uhg"])
            with nc.allow_non_contiguous_dma(reason="tiny conv state outputs"):
                for jj in range(2):
                    T.dma("sp", convp[lc, jj].rearrange("(cc p) -> p cc", p=128), uh[:, :, 7, jj], reads=uhkeys)
                    T.dma("sp", convs[lc, jj].rearrange("(cc p) -> p cc", p=128), uhs[:, :, jj],
                          reads=[("uhs", cc) for cc in range(KC)])
            hv = halo[:].rearrange("p a b c -> p (a b c)")
            T.op("dve", lambda e: e.tensor_scalar(out=hv, in0=uhg[:, 0, :], scalar1=hsel[:, 0:1], scalar2=None,
                                                   op0=ALU.mult), reads=["uhg", "hsel"], writes=["halo"])
            for r in range(1, 4):
                T.op("dve", lambda e, r=r: e.scalar_tensor_tensor(
                    out=hv, in0=uhg[:, r, :], scalar=hsel[:, r:r + 1], in1=hv, op0=ALU.mult, op1=ALU.add),
                    reads=["uhg", "hsel", "halo"], writes=["halo"])
            uhg3 = uhg[:, 3, :].rearrange("p (a b c) -> p a b c", a=KC, b=8)
            T.op("dve", lambda e: e.scalar_tensor_tensor(
                out=halo[:, :, 1:8, :], in0=uhg3[:, :, 0:7, :], scalar=hsel[:, 4:5], in1=halo[:, :, 1:8, :],
                op0=ALU.mult, op1=ALU.add), reads=["uhg", "hsel", "halo"], writes=["halo"])
            for wt in range(4):
                wbb, kb_ = wload_k16(W, wt * 512, 512)
                for ct in range(4):
                    cc = wt * 4 + ct
                    ukeys = [("uT", cc, g_) for g_ in range(3)]
                    T.op("dve", lambda e, cc=cc: e.tensor_copy(
                        out=uext[:, :, 2:130], in_=uT[:, cc, 0:NTP].rearrange("p (a b) -> p a b", a=8)),
                        reads=ukeys, writes=["uext"])
                    T.op("dve", lambda e, cc=cc: e.tensor_copy(out=uext[:, :, 0:2], in_=halo[:, cc, :, :]),
                         reads=["halo", "uext"], writes=["uext"])
                    T.op("dve", lambda e, cc=cc: e.tensor_copy(out=uexs[:, 2:10], in_=uT[:, cc, NTP:NT]),
                         reads=ukeys, writes=["uexs"])
                    T.op("dve", lambda e, cc=cc: e.tensor_copy(out=uexs[:, 0:2], in_=shs[:, cc, :]),
                         reads=[("shs", 0), ("shs", 1), "uexs"], writes=["uexs"])
                    v3 = v32[:, 0:NTP].rearrange("p (a b) -> p a b", a=8)
                    T.op("dve", lambda e, cc=cc: e.tensor_scalar(
                        out=v3, in0=uext[:, :, 2:130], scalar1=kcol[:, cc, 2:3], scalar2=None, op0=ALU.mult),
                        reads=["uext", ("kcol", 0), ("kcol", 1), ("kcol", 2)], writes=["v32"])
                    for jj, off in ((1, 1), (0, 0)):
                        T.op("dve", lambda e, cc=cc, jj=jj, off=off: e.scalar_tensor_tensor(
                            out=v3, in0=uext[:, :, off:off + 128], scalar=kcol[:, cc, jj:jj + 1], in1=v3,
                            op0=ALU.mult, op1=ALU.add), reads=["uext", ("kcol", 0), ("kcol", 1), ("kcol", 2), "v32"], writes=["v32"])
                    vs_ = v32[:, NTP:NT]
                    T.op("dve", lambda e, cc=cc: e.tensor_scalar(
                        out=vs_, in0=uexs[:, 2:10], scalar1=kcol[:, cc, 2:3], scalar2=None, op0=ALU.mult),
                        reads=["uexs", ("kcol", 0), ("kcol", 1), ("kcol", 2), "v32"], writes=["v32"])
                    for jj, off in ((1, 1), (0, 0)):
                        T.op("dve", lambda e, cc=cc, jj=jj, off=off: e.scalar_tensor_tensor(
                            out=vs_, in0=uexs[:, off:off + 8], scalar=kcol[:, cc, jj:jj + 1], in1=vs_,
                            op0=ALU.mult, op1=ALU.add), reads=["uexs", ("kcol", 0), ("kcol", 1), ("kcol", 2), "v32"], writes=["v32"])
                    for g_, (t0, tn) in enumerate(TG):
                        bb = cnt % 4
                        cnt += 1
                        mm_fm(bb, wbb, kb_, ct * 128, 128, hrhs, KC, t0, tn, HT_KEYS)
                        T.op("dve", lambda e, bb=bb, cc=cc, t0=t0, tn=tn: e.tensor_tensor(
                            out=uT[:, cc, t0:t0 + tn], in0=v32[:, t0:t0 + tn], in1=ps[bb][:, 0:tn], op=ALU.mult),
                            reads=["v32", ("ps", bb)], writes=[("uT", cc, g_)])
            allu = [("uT", cc, g_) for cc in range(KC) for g_ in range(3)]
            out_proj(conv_w_out[lc], lambda k, t0, tn: uT[:, k, t0:t0 + tn], allu)
            T.barrier()


    def sample_attn(l, qs, gates, pep, w1b, w2b, PM, PMK, ktnew, vnew):
        ccache = cache_cmp
        scache = cache_sel
        with ExitStack() as es:
            pt_i = tmp(es, "pt_i", [128, NPAGE], I32)
            pt_f = tmp(es, "pt_f", [128, NPAGE])
            piota = tmp(es, "piota", [128, 1])
            pidx = tmp(es, "pidx", [128, NPAGE], I32)
            pgst = [tmp(es, "pgst", [128, 1024]) for _ in range(2)]
            pgb = [tmp(es, "pgb", [128, 1024], BF16) for _ in range(1)] * 2
            cKTs = tmp(es, "cKTs", [128, 4, 1024], BF16)
            cVs = tmp(es, "cVs", [128, 8, 4, 128], BF16)
            ovs = tmp(es, "ovs", [128, 34])
            vms = tmp(es, "vms", [8, 257])
            fbs = tmp(es, "fbs", [8, 257])
            d8 = tmp(es, "d8", [8, 32], BF16)
            tri8 = tmp(es, "tri8", [8, 32], BF16)
            wins0 = tmp(es, "wins0", [128, 32], BF16)
            onesf = tmp(es, "onesf", [8, 128])
            selbS = tmp(es, "selbS", [8, 4, 257], BF16)
            OcS = tmp(es, "OcS", [128, 4, 64])
            T.dma("sp", pt_i[:], page_tab[0, :].partition_broadcast(128), writes=["pt_i"])
            T.dma("sp", piota[:], c_piota[:, :], writes=["piota"])
            T.dma("sp", ovs[:], c_ovs[:, :], writes=["ovs"])
            T.dma("sp", vms[:], c_vms[:, :], writes=["vms"])
            T.dma("sp", fbs[:], c_fbs[:, :], writes=["fbs"])
            T.dma("sp", d8[:], c_d8[:, :], writes=["d8"])
            T.dma("sp", tri8[:], c_tri8[:, :], writes=["tri8"])
            T.dma("sp", wins0[:], c_wins0[:, :], writes=["wins0"])
            T.op("dve", lambda e: e.memset(onesf[:], 1.0), writes=["onesf"])
            T.op("dve", lambda e: e.tensor_copy(out=pt_f[:], in_=pt_i[:]), reads=["pt_i"], writes=["pt_f"])
            T.op("dve", lambda e: e.tensor_scalar(out=pt_f[:], in0=pt_f[:], scalar1=128.0, scalar2=piota[:, 0:1],
                                                   op0=ALU.mult, op1=ALU.add),
                 reads=["pt_f", "piota"], writes=["pt_f"])
            T.op("dve", lambda e: e.tensor_scalar(out=pt_f[:], in0=pt_f[:], scalar1=float(l * 1280 * 128), scalar2=None,
                                                   op0=ALU.add), reads=["pt_f"], writes=["pt_f"])
            T.op("dve", lambda e: e.tensor_copy(out=pidx[:], in_=pt_f[:]), reads=["pt_f"], writes=["pidx"])

            with ExitStack() as es2:
                pooledS = tmp(es2, "pooledS", [128, 8, 1024])
                pSb = tmp(es2, "pSb", [128, 1024], BF16)
                hidS = tmp(es2, "hidS", [128, 1024], BF16)
                for pg in range(NPAGE):
                    bi = pg % 2
                    pgi = pg % 4
                    T.gather(pgst[bi][:, :], ccache[:, :], pidx[:, pg:pg + 1], reads=["pidx"], writes=[("pgst", bi)])
                    T.op("act" if pg % 2 else "dve",
                         (lambda e, bi=bi: e.copy(out=pgb[bi][:], in_=pgst[bi][:])) if pg % 2 else
                         (lambda e, bi=bi: e.tensor_copy(out=pgb[bi][:], in_=pgst[bi][:])),
                         reads=[("pgst", bi)], writes=[("pgb", 0)])
                    for gk in range(8):
                        T.op("pe", lambda e, gk=gk, bi=bi, pgi=pgi: e.matmul(
                            ps[6][:, gk * 64 + pgi * 16:gk * 64 + pgi * 16 + 16],
                            lhsT=pgb[bi][:, gk * 128:(gk + 1) * 128], rhs=PM[:, gk % 2, :], start=True, stop=True),
                            reads=[("pgb", 0)] + PMK, writes=[("ps", 6)])
                    if pgi == 3:
                        c0 = (pg - 3) * 8
                        bv = ps[6][:].rearrange("p (a b c) -> p a b c", a=8, b=4)
                        T.op("dve", lambda e, c0=c0, bv=bv: e.tensor_copy(
                            out=pooledS[:, :, c0:c0 + 32].rearrange("p a (b c) -> p a b c", b=4),
                            in_=bv[:, :, :, 0:8]), reads=[("ps", 6)], writes=["pooledS"])
                        for q in range(4):
                            cq = c0 + q * 8
                            if cq == 0:
                                T.op("dve", lambda e, bv=bv: e.tensor_tensor(
                                    out=pooledS[:, :, 0:7], in0=pooledS[:, :, 0:7], in1=bv[:, :, 0, 9:16], op=ALU.add),
                                    reads=[("ps", 6), "pooledS"], writes=["pooledS"])
                            else:
                                T.op("dve", lambda e, bv=bv, cq=cq, q=q: e.tensor_tensor(
                                    out=pooledS[:, :, cq - 1:cq + 7], in0=pooledS[:, :, cq - 1:cq + 7],
                                    in1=bv[:, :, q, 8:16], op=ALU.add),
                                    reads=[("ps", 6), "pooledS"], writes=["pooledS"])
                for g in range(4):
                    for kv in range(2):
                        gk = g * 2 + kv
                        T.op("dve", lambda e, gk=gk, kv=kv: e.tensor_scalar(
                            out=pSb[:], in0=pooledS[:, gk, :], scalar1=pep[:, kv:kv + 1], scalar2=None, op0=ALU.add),
                            reads=["pooledS", "pep"], writes=["pSb"])
                        for hf in range(2):
                            T.op("pe", lambda e, kv=kv, hf=hf: e.matmul(
                                ps[hf][:], lhsT=w1b[:, kv, :], rhs=pSb[:, hf * 512:(hf + 1) * 512], start=True, stop=True),
                                reads=["w1b", "pSb"], writes=[("ps", hf)])
                            T.op("act", lambda e, hf=hf: e.activation(
                                out=hidS[:, hf * 512:(hf + 1) * 512], in_=ps[hf][:], func=AF.Silu),
                                reads=[("ps", hf)], writes=[("hidS", hf)])
                        if kv == 0:
                            for hf in range(2):
                                T.op("pe", lambda e, hf=hf: e.matmul(
                                    ps[2 + hf][:], lhsT=w2b[:, 0, :], rhs=hidS[:, hf * 512:(hf + 1) * 512],
                                    start=True, stop=True), reads=["w2b", ("hidS", hf)], writes=[("ps", 2 + hf)])
                                T.op("dve", lambda e, hf=hf, g=g: e.tensor_copy(
                                    out=cKTs[:, g, hf * 512:(hf + 1) * 512], in_=ps[2 + hf][:]),
                                    reads=[("ps", 2 + hf)], writes=["cKTs"])
                        else:
                            for hf in range(2):
                                for q in range(4):
                                    tl = hf * 4 + q
                                    T.op("pe", lambda e, hf=hf, q=q, tl=tl: e.matmul(
                                        ps[2 + hf][:, q * 128:(q + 1) * 128], lhsT=hidS[:, tl * 128:(tl + 1) * 128],
                                        rhs=w2b[:, 1, :], start=True, stop=True),
                                        reads=["w2b", ("hidS", hf)], writes=[("ps", 2 + hf)])
                                T.op("dve", lambda e, hf=hf, g=g: e.tensor_copy(
                                    out=cVs[:, hf * 4:hf * 4 + 4, g, :],
                                    in_=ps[2 + hf][:].rearrange("p (a b) -> p a b", a=4)),
                                    reads=[("ps", 2 + hf)], writes=["cVs"])
                T.barrier()

            with ExitStack() as es3:
                P32s = tmp(es3, "P32s", [128, 8, 32])
                PbS = tmp(es3, "PbS", [128, 8, 32], BF16)
                rinv = tmp(es3, "rinv", [128, 32])
                Pr = tmp(es3, "Pr", [128, 8, 8])
                impS = tmp(es3, "impS", [8, 257])
                scS = tmp(es3, "scS", [8, 257])
                scS2 = tmp(es3, "scS2", [8, 257])
                m8s = tmp(es3, "m8s", [8, 16])
                t1s = tmp(es3, "t1s", [8, 257])
                t2s = tmp(es3, "t2s", [8, 257])
                for g in range(4):
                    for tl in range(8):
                        n = 128 if tl < 7 else 127
                        T.op("pe", lambda e, tl=tl, n=n, g=g: e.matmul(
                            ps[0][0:n, tl * 32:(tl + 1) * 32], lhsT=cKTs[:, g, tl * 128:tl * 128 + n],
                            rhs=qs[:, g, :], start=True, stop=True),
                            reads=["cKTs", ("qT", g, 8)], writes=[("ps", 0)])
                    T.op("act", lambda e: e.activation(out=P32s[:].rearrange("p a b -> p (a b)"), in_=ps[0][:, 0:256],
                                                       func=AF.Exp, scale=SCALE),
                         reads=[("ps", 0)], writes=["P32s"])
                    T.op("dve", lambda e: e.tensor_copy(out=PbS[:], in_=P32s[:]), reads=["P32s"], writes=["PbS"])
                    for tl in range(8):
                        n = 128 if tl < 7 else 127
                        T.op("pe", lambda e, tl=tl, n=n, g=g: e.matmul(
                            ps[1][:, 0:32], lhsT=cVs[0:n, tl, g, :], rhs=PbS[0:n, tl, :],
                            start=(tl == 0), stop=(tl == 7)), reads=["cVs", "PbS"], writes=[("ps", 1)])
                    for tl in range(8):
                        n = 128 if tl < 7 else 127
                        T.op("pe", lambda e, tl=tl, n=n: e.matmul(
                            ps[1][:, 32:64], lhsT=onesb[0:n, :], rhs=PbS[0:n, tl, :],
                            start=(tl == 0), stop=(tl == 7)), reads=["onesb", "PbS"], writes=[("ps", 1)])
                    T.op("dve", lambda e, g=g: e.tensor_copy(out=OcS[:, g, :], in_=ps[1][:, 0:64]),
                         reads=[("ps", 1)], writes=[("OcS", g)])
                    T.op("dve", lambda e, g=g: e.tensor_scalar(out=rinv[:], in0=OcS[:, g, 32:64], scalar1=1e-30,
                                                               scalar2=None, op0=ALU.max),
                         reads=[("OcS", g)], writes=["rinv"])
                    T.op("dve", lambda e: e.reciprocal(out=rinv[:], in_=rinv[:]), reads=["rinv"], writes=["rinv"])
                    T.op("dve", lambda e: e.tensor_tensor(
                        out=P32s[:], in0=P32s[:], in1=rinv[:].unsqueeze(1).broadcast_to([128, 8, 32]), op=ALU.mult),
                        reads=["P32s", "rinv"], writes=["P32s"])
                    T.op("dve", lambda e: e.tensor_reduce(
                        out=Pr[:], in_=P32s[:].rearrange("p t (r k) -> p t k r", r=4), axis=AX.X, op=ALU.add),
                        reads=["P32s"], writes=["Pr"])
                    for tl in range(8):
                        n = 128 if tl < 7 else 127
                        T.op("pe", lambda e, tl=tl, n=n: e.matmul(
                            ps[2][0:8, tl * 34:(tl + 1) * 34], lhsT=Pr[0:n, tl, :], rhs=ovs[0:n, :],
                            start=True, stop=True), reads=["Pr", "ovs"], writes=[("ps", 2)])
                    iv = ps[2][0:8, 0:272].rearrange("p (a b) -> p a b", a=8)
                    T.op("dve", lambda e: e.memset(impS[:, 256:257], 0.0), writes=["impS"])
                    T.op("dve", lambda e, iv=iv: e.tensor_copy(
                        out=impS[:, 0:256].rearrange("p (a b) -> p a b", a=8), in_=iv[:, :, 0:32]),
                        reads=[("ps", 2), "impS"], writes=["impS"])
                    T.op("dve", lambda e, iv=iv: e.tensor_tensor(
                        out=impS[:, 32:257:32], in0=impS[:, 32:257:32], in1=iv[:, :, 32], op=ALU.add),
                        reads=[("ps", 2), "impS"], writes=["impS"])
                    T.op("dve", lambda e: e.tensor_tensor(out=scS[:], in0=impS[:], in1=vms[:], op=ALU.mult),
                         reads=["impS", "vms"], writes=["scS"])
                    T.op("dve", lambda e: e.tensor_tensor(out=scS[:], in0=scS[:], in1=fbs[:], op=ALU.add),
                         reads=["scS", "fbs"], writes=["scS"])
                    T.op("dve", lambda e: e.max(out=m8s[:, 0:8], in_=scS[:]), reads=["scS"], writes=["m8s"])
                    T.op("dve", lambda e: e.match_replace(out=scS2[:], in_to_replace=m8s[:, 0:8], in_values=scS[:],
                                                           imm_value=-3e38), reads=["scS", "m8s"], writes=["scS2"])
                    T.op("dve", lambda e: e.max(out=m8s[:, 8:16], in_=scS2[:]), reads=["scS2", "m8s"], writes=["m8s"])
                    T.op("dve", lambda e: e.tensor_scalar(out=t1s[:], in0=scS[:], scalar1=m8s[:, 15:16], scalar2=None,
                                                           op0=ALU.is_ge), reads=["scS", "m8s"], writes=["t1s"])
                    T.op("dve", lambda e: e.tensor_scalar(out=t2s[:], in0=scS[:], scalar1=-5e29, scalar2=None,
                                                           op0=ALU.is_gt), reads=["scS"], writes=["t2s"])
                    T.op("dve", lambda e: e.tensor_tensor(out=t1s[:], in0=t1s[:], in1=t2s[:], op=ALU.mult),
                         reads=["t1s", "t2s"], writes=["t1s"])
                    T.op("dve", lambda e, g=g: e.tensor_scalar(out=selbS[:, g, :], in0=t1s[:], scalar1=-1.0, scalar2=BIG,
                                                               op0=ALU.add, op1=ALU.mult),
                         reads=["t1s"], writes=[("selbS", g)])
                T.barrier()

            with ExitStack() as es4:
                kpg = [tmp(es4, "kpg", [128, 4, 128], BF16) for _ in range(2)]
                vpg = [tmp(es4, "vpg", [128, 4, 128], BF16) for _ in range(2)]
                ktp = [tmp(es4, "ktp", [128, 4, 128], BF16) for _ in range(2)]
                sbx = [tmp(es4, "sbx", [8, 4, 2, 64], BF16) for _ in range(2)]
                Ppg = [tmp(es4, "Ppg", [128, 4, 32], BF16) for _ in range(2)]
                Pn8 = tmp(es4, "Pn8", [8, 4, 32], BF16)
                Grep = tmp(es4, "Grep", [128, 48, 8])
                gexp = tmp(es4, "gexp", [8, 48, 8])
                Osb = tmp(es4, "Osb", [128, 256])
                wv = tmp(es4, "wv", [128, 128])
                acc = tmp(es4, "acc", [128, 128])
                selk = [("selbS", g) for g in range(4)]
                qk = [("qT", g, 8) for g in range(4)]

                def key_tile(it, src_kind, src_idx, bank_acc, first, maskfn):
                    bi = it % 2
                    if first:
                        T.op("dve", lambda e: e.memset(ps[bank_acc][:, 0:256], 0.0), writes=[("ps", bank_acc)])
                    first = False
                    if src_kind == "page":
                        T.gather(pgst[bi][:, :], scache[:, :], pidx[:, src_idx:src_idx + 1], reads=["pidx"],
                                 writes=[("pgst", bi)])
                    else:
                        T.dma("sp", pgst[bi][:, :], state_win[l, src_idx * 128:(src_idx + 1) * 128, :],
                              writes=[("pgst", bi)])
                    pv4 = pgst[bi][:].rearrange("p (g k d) -> p g k d", g=4, k=2)
                    T.op("dve", lambda e, bi=bi, pv4=pv4: e.tensor_copy(out=kpg[bi][:], in_=pv4[:, :, 0, :]),
                         reads=[("pgst", bi)], writes=[("kpg", bi)])
                    T.op("act", lambda e, bi=bi, pv4=pv4: e.copy(out=vpg[bi][:], in_=pv4[:, :, 1, :]),
                         reads=[("pgst", bi)], writes=[("vpg", bi)])
                    for g in range(4):
                        T.op("pe", lambda e, g=g, bi=bi: e.transpose(out=psb[:, g * 128:(g + 1) * 128],
                                                                      in_=kpg[bi][:, g, :], identity=identb[:]),
                             reads=[("kpg", bi), "identb"], writes=["psb"])
                    T.op("act", lambda e, bi=bi: e.copy(out=ktp[bi][:].rearrange("p a b -> p (a b)"), in_=psb[:, 0:512]),
                         reads=["psb"], writes=[("ktp", bi)])
                    sb_ = it % 2
                    for g in range(4):
                        T.op("pe", lambda e, g=g, bi=bi, sb_=sb_: e.matmul(
                            ps[sb_][:, g * 32:(g + 1) * 32], lhsT=ktp[bi][:, g, :], rhs=qs[:, g, :],
                            start=True, stop=(maskfn is None)), reads=[("ktp", bi)] + qk, writes=[("ps", sb_)])
                        if maskfn is not None:
                            maskfn(g, bi, sb_)
                    T.op("act", lambda e, bi=bi, sb_=sb_: e.activation(
                        out=Ppg[bi][:].rearrange("p a b -> p (a b)"), in_=ps[sb_][:, 0:128], func=AF.Exp, scale=SCALE),
                        reads=[("ps", sb_)], writes=[("Ppg", bi)])
                    for g in range(4):
                        T.op("pe", lambda e, g=g, bi=bi: e.matmul(
                            ps[bank_acc][:, g * 32:(g + 1) * 32], lhsT=vpg[bi][:, g, :], rhs=Ppg[bi][:, g, :],
                            start=first, stop=False), reads=[("vpg", bi), ("Ppg", bi)], writes=[("ps", bank_acc)])
                        T.op("pe", lambda e, g=g, bi=bi: e.matmul(
                            ps[bank_acc][:, 128 + g * 32:128 + (g + 1) * 32], lhsT=onesb[:], rhs=Ppg[bi][:, g, :],
                            start=first, stop=False), reads=["onesb", ("Ppg", bi)], writes=[("ps", bank_acc)])

                def new_rows(brn, bank_acc):
                    for g in range(4):
                        T.op("pe", lambda e, g=g: e.matmul(
                            ps[2][0:8, g * 32:(g + 1) * 32], lhsT=ktnew[:, brn, g, :], rhs=qs[:, g, :],
                            start=True, stop=False), reads=[("ktnew", brn + 1, g)] + qk, writes=[("ps", 2)])
                        T.op("pe", lambda e, g=g: e.matmul(
                            ps[2][0:8, g * 32:(g + 1) * 32], lhsT=identb[0:8, 0:8], rhs=tri8[:],
                            start=False, stop=True), reads=["identb", "tri8"], writes=[("ps", 2)])
                    T.op("act", lambda e: e.activation(out=Pn8[:].rearrange("p a b -> p (a b)"), in_=ps[2][0:8, 0:128],
                                                       func=AF.Exp, scale=SCALE), reads=[("ps", 2)], writes=["Pn8"])
                    for g in range(4):
                        T.op("pe", lambda e, g=g: e.matmul(
                            ps[bank_acc][:, g * 32:(g + 1) * 32], lhsT=vnew[0:8, brn, g, :], rhs=Pn8[:, g, :],
                            start=False, stop=True), reads=[("vnew", brn + 1, 0), ("vnew", brn + 1, 2), "Pn8"],
                            writes=[("ps", bank_acc)])
                        T.op("pe", lambda e, g=g: e.matmul(
                            ps[bank_acc][:, 128 + g * 32:128 + (g + 1) * 32], lhsT=onesb[0:8, :], rhs=Pn8[:, g, :],
                            start=False, stop=True), reads=["onesb", "Pn8"], writes=[("ps", bank_acc)])

                for pg in range(NPAGE):
                    bi = pg % 2
                    T.op("dve", lambda e, pg=pg, bi=bi: e.tensor_copy(
                        out=sbx[bi][:], in_=selbS[:, :, 2 * pg:2 * pg + 2].unsqueeze(3).broadcast_to([8, 4, 2, 64])),
                        reads=selk, writes=[("sbx", bi)])

                    def mask_sel(g, bi, sb_):
                        T.op("pe", lambda e: e.matmul(
                            ps[sb_][:, g * 32:(g + 1) * 32], lhsT=sbx[bi][:, g].rearrange("p a b -> p (a b)"),
                            rhs=d8[:], start=False, stop=True), reads=[("sbx", bi), "d8"], writes=[("ps", sb_)])
                    key_tile(pg, "page", pg, 4, pg == 0, mask_sel)
                new_rows(0, 4)

                for wt_ in range(4):
                    def mask_win(g, bi, sb_):
                        T.op("pe", lambda e: e.matmul(
                            ps[sb_][:, g * 32:(g + 1) * 32], lhsT=identb[:], rhs=wins0[:],
                            start=False, stop=True), reads=["identb", "wins0"], writes=[("ps", sb_)])
                    key_tile(NPAGE + wt_, "win", wt_, 5, wt_ == 0, mask_win if wt_ == 0 else None)
                new_rows(1, 5)

                T.op("dve", lambda e: e.tensor_tensor(
                    out=gexp[:], in0=gates[0:8, 8, :].unsqueeze(2).broadcast_to([8, 48, 8]),
                    in1=identf[0:8, 0:8].unsqueeze(1).broadcast_to([8, 48, 8]), op=ALU.mult),
                    reads=[("gates", 8), "identf"], writes=["gexp"])
                T.op("pe", lambda e: e.matmul(ps[3][:, 0:384], lhsT=onesf[:], rhs=gexp[:].rearrange("p a b -> p (a b)"),
                                              start=True, stop=True), reads=["onesf", "gexp"], writes=[("ps", 3)])
                T.op("dve", lambda e: e.tensor_copy(out=Grep[:].rearrange("p a b -> p (a b)"), in_=ps[3][:, 0:384]),
                     reads=[("ps", 3)], writes=["Grep"])
                for br in range(3):
                    if br == 0:
                        Ov = OcS[:].rearrange("p g (a b) -> p g a b", a=2)
                        o_ap = Ov[:, :, 0, :]
                        r_ap = Ov[:, :, 1, :]
                        rk = [("OcS", g) for g in range(4)]
                    else:
                        T.op("dve", lambda e, br=br: e.tensor_copy(out=Osb[:], in_=ps[3 + br][:, 0:256]),
                             reads=[("ps", 3 + br)], writes=["Osb"])
                        o_ap = Osb[:, 0:128].rearrange("p (g b) -> p g b", g=4)
                        r_ap = Osb[:, 128:256].rearrange("p (g b) -> p g b", g=4)
                        rk = ["Osb"]
                    wv3 = wv[:].rearrange("p (g b) -> p g b", g=4)
                    T.op("dve", lambda e, r_ap=r_ap, wv3=wv3: e.tensor_scalar(out=wv3, in0=r_ap, scalar1=1e-30, scalar2=None,
                                                                            op0=ALU.max), reads=rk, writes=["wv"])
                    T.op("dve", lambda e: e.reciprocal(out=wv[:], in_=wv[:]), reads=["wv"], writes=["wv"])
                    T.op("dve", lambda e, br=br: e.tensor_tensor(
                        out=wv[:], in0=wv[:], in1=Grep[:, br * 16:(br + 1) * 16, :].rearrange("p a b -> p (a b)"),
                        op=ALU.mult), reads=["wv", "Grep"], writes=["wv"])
                    T.op("dve", lambda e, o_ap=o_ap, wv3=wv3: e.tensor_tensor(out=wv3, in0=o_ap, in1=wv3, op=ALU.mult),
                         reads=rk + ["wv"], writes=["wv"])
                    if br == 0:
                        T.op("dve", lambda e: e.tensor_copy(out=acc[:], in_=wv[:]), reads=["wv"], writes=["acc"])
                    else:
                        T.op("dve", lambda e: e.tensor_tensor(out=acc[:], in0=acc[:], in1=wv[:], op=ALU.add),
                             reads=["wv", "acc"], writes=["acc"])
                T.op("dve", lambda e: e.tensor_copy(out=qs[:].rearrange("p a b -> p (a b)"), in_=acc[:]),
                     reads=["acc"], writes=qk)
                T.barrier()

    import os as _os
    KLEVEL = int(_os.environ.get("KLEVEL", "99"))

    class _Stop(Exception):
        pass

    def ck(n):
        if n > KLEVEL:
            raise _Stop()

    def attn_layer(l):
        try:
            _attn_layer(l)
        except _Stop:
            T.barrier()

    def _attn_layer(l):
        Win = attn_w_in[l]
        with ExitStack() as es:
            qT = tmp(es, "qT", [128, 4, 8, 512], BF16)
            qs = tmp(es, "qs", [128, 4, 32], BF16)
            gates = tmp(es, "gates", [128, 9, 48])
            pwb = tmp(es, "pwb", [128, 4, 16])
            pwt = tmp(es, "pwt", [128, 4, 16])
            oh16 = tmp(es, "oh16", [128, 16])
            pmmask = tmp(es, "pmmask", [128, 16])
            wcol = tmp(es, "wcol", [128, 4])
            PM = tmp(es, "PM", [128, 2, 16], BF16)
            pe_sb = tmp(es, "pe_sb", [32, 2, 128])
            pw_sb = tmp(es, "pw_sb", [32, 2])
            pep = tmp(es, "pep", [128, 2])
            w1b = tmp(es, "w1b", [128, 2, 128], BF16)
            w2b = tmp(es, "w2b", [128, 2, 128], BF16)
            ktnew = tmp(es, "ktnew", [128, 2, 4, 8], BF16)
            vnew = tmp(es, "vnew", [8, 2, 4, 128], BF16)
            T.dma("sp", wins[l, 0:504, :], state_win[l, 8:512, :], writes=[("wins", l)])
            T.dma("sp", oh16[:], c_oh16[:, :], writes=["oh16"])
            T.dma("sp", pmmask[:], c_pmmask[:, :], writes=["pmmask"])
            T.dma("sp", pwb[:].rearrange("p a b -> p (a b)"),
                  cmp_pool[l].rearrange("k j -> (k j)").partition_broadcast(128), writes=["pwb"])
            T.dma("sp", pe_sb[:], cmp_pe[l].rearrange("k j d -> j k d"), writes=["pe_sb"])
            with nc.allow_non_contiguous_dma(reason="tiny pool weights"):
                T.dma("sp", pw_sb[:], cmp_pool[l].rearrange("k j -> j k"), writes=["pw_sb"])
            T.dma("pool", w1b[:], cmp_w1[l].rearrange("k d e -> d k e"), writes=["w1b"])
            T.dma("pool", w2b[:], cmp_w2[l].rearrange("k d e -> d k e"), writes=["w2b"])
            T.op("dve", lambda e: e.tensor_tensor(out=pwt[:], in0=pwb[:],
                                                   in1=oh16[:].unsqueeze(1).broadcast_to([128, 4, 16]), op=ALU.mult),
                 reads=["pwb", "oh16"], writes=["pwt"])
            T.op("dve", lambda e: e.tensor_reduce(out=wcol[:], in_=pwt[:], axis=AX.X, op=ALU.add),
                 reads=["pwt"], writes=["wcol"])
            for kv in range(2):
                for hf in range(2):
                    T.op("dve", lambda e, kv=kv, hf=hf: e.tensor_scalar(
                        out=PM[:, kv, hf * 8:(hf + 1) * 8], in0=pmmask[:, hf * 8:(hf + 1) * 8],
                        scalar1=wcol[:, kv * 2 + hf:kv * 2 + hf + 1], scalar2=None, op0=ALU.mult),
                        reads=["pmmask", "wcol"], writes=[("PM", kv, hf)])
            PMK = [("PM", kv, hf) for kv in range(2) for hf in range(2)]
            for kv in range(2):
                T.op("pe", lambda e, kv=kv: e.matmul(ps[6][:, kv:kv + 1], lhsT=pe_sb[:, kv, :],
                                                      rhs=pw_sb[:, kv:kv + 1], start=True, stop=True),
                     reads=["pe_sb", "pw_sb"], writes=[("ps", 6)])
            T.op("dve", lambda e: e.tensor_copy(out=pep[:], in_=ps[6][:, 0:2]), reads=[("ps", 6)], writes=["pep"])

            if 2 > KLEVEL:

                T.barrier()

                return
            with ExitStack() as es2:
                hT = tmp(es2, "hT", [128, KC, NT], BF16)
                P01 = tmp(es2, "P01", [128, 8, 8, 16])
                kvst = [tmp(es2, "kvst", [128, 512]) for _ in range(2)]
                kvb = [tmp(es2, "kvb", [128, 512], BF16) for _ in range(2)]
                kst = [tmp(es2, "kst", [128, 512], BF16) for _ in range(2)]
                rmsnorm(l, hT)
                hrhs = lambda k, t0, tn: hT[:, k, t0:t0 + tn]
                cnt = 0
                for g in range(4):
                    wb, wk = wload_k16(Win, g * 512, 512)
                    for h in range(4):
                        for g_, (t0, tn) in enumerate(TG):
                            b_ = cnt % 3
                            cnt += 1
                            mm_fm(b_, wb, wk, h * 128, 128, hrhs, KC, t0, tn, HT_KEYS)
                            if g_ < 2:
                                T.op("act", lambda e, b_=b_, g=g, h=h, g_=g_: e.copy(
                                    out=qT[:, g, 4 * g_:4 * g_ + 4, h * 128:(h + 1) * 128],
                                    in_=ps[b_][:].rearrange("p (a b) -> p a b", a=4)),
                                    reads=[("ps", b_)], writes=[("qT", g, j) for j in range(4 * g_, 4 * g_ + 4)])
                            else:
                                T.op("act", lambda e, b_=b_, g=g, h=h: e.copy(
                                    out=qs[:, g, h * 8:(h + 1) * 8], in_=ps[b_][:, 0:8]),
                                    reads=[("ps", b_)], writes=[("qT", g, 8)])
                if 3 > KLEVEL:
                    T.barrier()
                    return
                xgkeys = []
                for wt in range(6):
                    br = wt // 2
                    g0 = (wt % 2) * 2
                    wb, wk = wload_k16(Win, D + wt * 512, 512)
                    for t in range(9):
                        n = 128 if t < 8 else NS
                        b_ = cnt % 3
                        si = cnt % 2
                        cnt += 1
                        for k in range(KC):
                            T.op("pe", lambda e, k=k, b_=b_, t=t, n=n: e.matmul(
                                ps[b_][0:n, :], lhsT=hT[:, k, t * 128:t * 128 + n], rhs=wb[:, k, :],
                                start=(k == 0), stop=(k == KC - 1)),
                                reads=[wk, ("hT", k)], writes=[("ps", b_)])
                        T.op("act", lambda e, b_=b_, si=si, n=n: e.copy(out=kvst[si][0:n, :], in_=ps[b_][0:n, :]),
                             reads=[("ps", b_)], writes=[("kvst", si)])
                        if t < 8:
                            T.dma("sp", kvp[l, br, t * 128:(t + 1) * 128, (wt % 2) * 512:(wt % 2) * 512 + 512],
                                  kvst[si][:, :], reads=[("kvst", si)])
                        else:
                            T.dma("sp", kvs[l, br, :, (wt % 2) * 512:(wt % 2) * 512 + 512],
                                  kvst[si][0:NS, :], reads=[("kvst", si)])
                            if br == 2:
                                T.dma("sp", wins[l, 504:512, (wt % 2) * 512:(wt % 2) * 512 + 512],
                                      kvst[si][0:NS, :], reads=[("kvst", si)])
                        T.op("dve", lambda e, b_=b_, si=si, n=n: e.tensor_copy(out=kvb[si][0:n, :], in_=ps[b_][0:n, :]),
                             reads=[("ps", b_)], writes=[("kvb", si)])
                        if br == 0:
                            if t < 8:
                                for q4 in range(4):
                                    kv = q4 % 2
                                    T.op("pe", lambda e, q4=q4, kv=kv, si=si: e.matmul(
                                        ps[6][:, q4 * 16:(q4 + 1) * 16], lhsT=kvb[si][:, q4 * 128:(q4 + 1) * 128],
                                        rhs=PM[:, kv, :], start=True, stop=True),
                                        reads=[("kvb", si)] + PMK, writes=[("ps", 6)])
                                T.op("dve", lambda e, g0=g0, t=t: e.tensor_copy(
                                    out=P01[:, g0 * 2:g0 * 2 + 4, t, :],
                                    in_=ps[6][:, 0:64].rearrange("p (a b) -> p a b", a=4)),
                                    reads=[("ps", 6)], writes=[("P01", g0, t)])
                        else:
                            if t < 8:
                                key = ("xg_in", "v", br, wt % 2, t)
                                xgkeys.append(key)
                                vrow0 = t * 128
                                T.dma("sp", xgv_in[br - 1][vrow0:vrow0 + 128, g0 * 128:g0 * 128 + 256].rearrange(
                                    "p (a b) -> p a b", a=2),
                                    kvb[si][:].rearrange("p (a b) -> p a b", a=4)[:, 1::2, :],
                                    reads=[("kvb", si)], writes=[key])
                            else:
                                T.op("dve", lambda e, si=si, br=br, g0=g0: e.tensor_copy(
                                    out=vnew[:, br - 1, g0:g0 + 2, :],
                                    in_=kvb[si][0:NS, :].rearrange("p (a b) -> p a b", a=4)[:, 1::2, :]),
                                    reads=[("kvb", si)], writes=[("vnew", br, g0)])
                    if br >= 1:
                        for gg in range(2):
                            g = g0 + gg
                            for g_, (t0, tn) in enumerate(TG):
                                b_ = cnt % 3
                                si = cnt % 2
                                cnt += 1
                                mm_fm(b_, wb, wk, gg * 256, 128, hrhs, KC, t0, tn, HT_KEYS)
                                if g_ < 2:
                                    T.op("act", lambda e, b_=b_, si=si: e.copy(out=kst[si][:], in_=ps[b_][:]),
                                         reads=[("ps", b_)], writes=[("kst", si)])
                                    key = ("xg_in", "k", br, g, g_)
                                    xgkeys.append(key)
                                    dst = xgk_in[br - 1].rearrange("(d a) c -> d (a c)", d=128)
                                    T.dma("sp", dst[:, g * 1024 + g_ * 512:g * 1024 + g_ * 512 + 512], kst[si][:],
                                          reads=[("kst", si)], writes=[key])
                                else:
                                    T.op("act", lambda e, b_=b_, br=br, g=g: e.copy(
                                        out=ktnew[:, br - 1, g, :], in_=ps[b_][:, 0:8]),
                                        reads=[("ps", b_)], writes=[("ktnew", br, g)])
                wb, wk = wload_k16(Win, 5120, 48)
                for t in range(9):
                    n = 128 if t < 8 else NS
                    b_ = cnt % 3
                    cnt += 1
                    for k in range(KC):
                        T.op("pe", lambda e, k=k, b_=b_, t=t, n=n: e.matmul(
                            ps[b_][0:n, 0:48], lhsT=hT[:, k, t * 128:t * 128 + n], rhs=wb[:, k, 0:48],
                            start=(k == 0), stop=(k == KC - 1)),
                            reads=[wk, ("hT", k)], writes=[("ps", b_)])
                    T.op("act", lambda e, b_=b_, t=t, n=n: e.activation(
                        out=gates[0:n, t, :], in_=ps[b_][0:n, 0:48], func=AF.Sigmoid),
                        reads=[("ps", b_)], writes=[("gates", t)])
                if 4 > KLEVEL:
                    T.barrier()
                    return
                p01keys = [("P01", g0, t) for g0 in (0, 2) for t in range(8)]
                T.dma("sp", pg_in[:, :], P01[:].rearrange("p a b c -> p (a b c)"), reads=p01keys, writes=["pg_in"])
                import os
                dbg2 = os.environ.get("KDEBUG", "")
                for bi in range(2):
                    T.collective("AllGather", GROUPS, xgk_in[bi].opt(), xgk_out[bi].opt(),
                                 reads=[k for k in xgkeys if k[1] == "k" and k[2] == bi + 1], writes=[("xgk_out", bi)])
                    T.collective("AllGather", GROUPS, xgv_in[bi].opt(), xgv_out[bi].opt(),
                                 reads=[k for k in xgkeys if k[1] == "v" and k[2] == bi + 1], writes=[("xgv_out", bi)])
                if "nocc2" in dbg2:
                    T.dma("sp", pg_out[0:128, :], pg_in[:, :], reads=["pg_in"], writes=["pg_out"])
                else:
                    T.collective("AllGather", GROUPS, pg_in.opt(), pg_out.opt(), reads=["pg_in"], writes=["pg_out"])
                T.barrier()

            if 5 > KLEVEL:

                T.barrier()

                return
            with ExitStack() as es3:
                cKT = tmp(es3, "cKT", [128, 4, 256], BF16)
                cV = tmp(es3, "cV", [128, 2, 4, 129], BF16)
                with ExitStack() as es4:
                    P01g = tmp(es4, "P01g", [128, 4, 8, 8, 16])
                    pooled = tmp(es4, "pooled", [128, 8, 256])
                    pooledb = tmp(es4, "pooledb", [128, 8, 256], BF16)
                    hidb = tmp(es4, "hidb", [128, 256], BF16)

                    T.dma("sp", P01g[:].rearrange("p r a b c -> p r (a b c)"),
                          pg_out.rearrange("(r p) n -> p r n", p=128), writes=["P01g"])
                    T.op("dve", lambda e: e.memset(cV[:], 1.0), writes=["cV"])
                    pv = pooled[:].rearrange("p a (t r c) -> p a t r c", t=8, r=4)
                    for r in range(4):
                        T.op("dve", lambda e, r=r: e.tensor_copy(out=pv[:, :, :, r, :], in_=P01g[:, r, :, :, 0:8]),
                             reads=["P01g"], writes=["pooled"])
                    for r in range(4):
                        T.op("dve", lambda e, r=r: e.tensor_tensor(
                            out=pv[:, :, :, r, 0:7], in0=pv[:, :, :, r, 0:7], in1=P01g[:, r, :, :, 9:16], op=ALU.add),
                            reads=["P01g", "pooled"], writes=["pooled"])
                    for r in range(3):
                        T.op("dve", lambda e, r=r: e.tensor_tensor(
                            out=pv[:, :, :, r, 7], in0=pv[:, :, :, r, 7], in1=P01g[:, r + 1, :, :, 8], op=ALU.add),
                            reads=["P01g", "pooled"], writes=["pooled"])
                    T.op("dve", lambda e: e.tensor_tensor(
                        out=pv[:, :, 0:7, 3, 7], in0=pv[:, :, 0:7, 3, 7], in1=P01g[:, 0, :, 1:8, 8], op=ALU.add),
                        reads=["P01g", "pooled"], writes=["pooled"])
                    pkv = pooled[:].rearrange("p (g k) n -> p g k n", k=2)
                    pbkv = pooledb[:].rearrange("p (g k) n -> p g k n", k=2)
                    for kv in range(2):
                        T.op("dve", lambda e, kv=kv: e.tensor_scalar(
                            out=pbkv[:, :, kv, :], in0=pkv[:, :, kv, :], scalar1=pep[:, kv:kv + 1], scalar2=None,
                            op0=ALU.add), reads=["pooled", "pep"], writes=[("pooledb", kv)])
                    for g in range(4):
                        for kv in range(2):
                            gk = g * 2 + kv
                            T.op("pe", lambda e, gk=gk, kv=kv: e.matmul(
                                ps[5][:, 0:256], lhsT=w1b[:, kv, :], rhs=pooledb[:, gk, :], start=True, stop=True),
                                reads=["w1b", ("pooledb", kv)], writes=[("ps", 5)])
                            T.op("act", lambda e: e.activation(out=hidb[:], in_=ps[5][:, 0:256], func=AF.Silu),
                                 reads=[("ps", 5)], writes=["hidb"])
                            if kv == 0:
                                T.op("pe", lambda e: e.matmul(ps[6][:, 0:256], lhsT=w2b[:, 0, :], rhs=hidb[:],
                                                              start=True, stop=True),
                                     reads=["w2b", "hidb"], writes=[("ps", 6)])
                                T.op("dve", lambda e, g=g: e.tensor_copy(out=cKT[:, g, :], in_=ps[6][:, 0:256]),
                                     reads=[("ps", 6)], writes=["cKT"])
                            else:
                                for tl in range(2):
                                    T.op("pe", lambda e, tl=tl: e.matmul(
                                        ps[6][:, tl * 128:(tl + 1) * 128], lhsT=hidb[:, tl * 128:(tl + 1) * 128],
                                        rhs=w2b[:, 1, :], start=True, stop=True),
                                        reads=["w2b", "hidb"], writes=[("ps", 6)])
                                T.op("dve", lambda e, g=g: e.tensor_copy(
                                    out=cV[:, :, g, 0:128], in_=ps[6][:, 0:256].rearrange("p (a b) -> p a b", a=2)),
                                    reads=[("ps", 6)], writes=["cV"])


                    T.barrier()
                if 6 > KLEVEL:
                    T.barrier()
                    return
                KTs = tmp(es3, "KTs", [128, 32, 128], BF16)
                KTw = tmp(es3, "KTw", [128, 32, 128], BF16)
                Vs = tmp(es3, "Vs", [128, 32, 129], BF16)
                Vw = tmp(es3, "Vw", [128, 32, 129], BF16)
                cmpm = tmp(es3, "cmpm", [128, 2, 128], BF16)
                diagm = tmp(es3, "diagm", [128, 4, 128], BF16)
                winm = tmp(es3, "winm", [128, 8, 128], BF16)
                vm = tmp(es3, "vm", [128, 64])
                fb = tmp(es3, "fb", [128, 64])
                ov = tmp(es3, "ov", [128, 2, 65])
                P32 = [tmp(es3, "P32", [128, 512]) for _ in range(2)]
                Pb = [tmp(es3, "Pb", [128, 512], BF16) for _ in range(3)]
                imp = tmp(es3, "imp", [128, 64])
                score = tmp(es3, "score", [128, 64])
                score2 = tmp(es3, "score2", [128, 64])
                m8 = tmp(es3, "m8", [128, 16])
                t1 = tmp(es3, "t1", [128, 64])
                t2 = tmp(es3, "t2", [128, 64])
                selb = tmp(es3, "selb", [128, 64], BF16)
                selbx = tmp(es3, "selbx", [128, 64, 64], BF16)
                rsx = tmp(es3, "rsx", [128, 4])
                coef = tmp(es3, "coef", [128, 4])
                mix = tmp(es3, "mix", [128, 4, 128])
                mixb = tmp(es3, "mixb", [128, 4, 128], BF16)
                T.dma("sp", diagm[:], c_diagmask[:, :, :], writes=["diagm"])
                T.dma("sp", winm[:], c_winmask[:, :, :], writes=["winm"])
                T.dma("sp", ov[:], c_ov[:, :, :], writes=["ov"])
                T.op("dve", lambda e: e.memset(Vs[:], 1.0), writes=["Vs"])
                T.op("dve", lambda e: e.memset(Vw[:], 1.0), writes=["Vw"])
                sctr = {"s": 0, "p": 0}

                def s_bank():
                    sctr["s"] += 1
                    return sctr["s"] % 3

                def p_buf():
                    sctr["p"] += 1
                    return sctr["p"] % 3

                def pv_accum(pbi, vten, vkeys, vsl, first, last):
                    if first:
                        for hp in range(2):
                            T.op("dve", lambda e, hp=hp: e.memset(ps[3 + hp][:, 0:258], 0.0), writes=[("ps", 3 + hp)])
                    first = False
                    for h in range(4):
                        T.op("pe", lambda e, h=h: e.matmul(
                            ps[3 + h // 2][:, (h % 2) * 129:(h % 2) * 129 + 129],
                            lhsT=Pb[pbi][:, h * 128:(h + 1) * 128], rhs=vsl, start=first, stop=last),
                            reads=[("Pb", pbi)] + vkeys, writes=[("ps", 3), ("ps", 4)])

                def merge(br, j, g):
                    for hp in range(2):
                        T.op("dve", lambda e, hp=hp: e.tensor_scalar(
                            out=rsx[:, hp * 2:hp * 2 + 2],
                            in0=ps[3 + hp][:, 0:258].rearrange("p (a b) -> p a b", a=2)[:, :, 128],
                            scalar1=1e-30, scalar2=None, op0=ALU.max),
                            reads=[("ps", 3 + hp)], writes=["rsx"])
                    T.op("dve", lambda e: e.reciprocal(out=rsx[:], in_=rsx[:]), reads=["rsx"], writes=["rsx"])
                    T.op("dve", lambda e: e.tensor_tensor(
                        out=coef[:], in0=rsx[:], in1=gates[:, j, br * 16 + g * 4:br * 16 + g * 4 + 4], op=ALU.mult),
                        reads=["rsx", ("gates", j)], writes=["coef"])
                    for h in range(4):
                        src = ps[3 + h // 2][:, (h % 2) * 129:(h % 2) * 129 + 128]
                        if br == 0:
                            T.op("dve", lambda e, h=h, src=src: e.tensor_scalar(
                                out=mix[:, h, :], in0=src, scalar1=coef[:, h:h + 1], scalar2=None, op0=ALU.mult),
                                reads=[("ps", 3 + h // 2), "coef"], writes=[("mix", h)])
                        else:
                            T.op("dve", lambda e, h=h, src=src: e.scalar_tensor_tensor(
                                out=mix[:, h, :], in0=src, scalar=coef[:, h:h + 1], in1=mix[:, h, :],
                                op0=ALU.mult, op1=ALU.add),
                                reads=[("ps", 3 + h // 2), "coef", ("mix", h)], writes=[("mix", h)])

                for g in range(4):
                    for r in range(4):
                        for bi, (KTt, Vt, nm) in enumerate(((KTs, Vs, "s"), (KTw, Vw, "w"))):
                            r0 = r * 1024
                            src = xgk_out[bi][r0:r0 + 1024, :].rearrange("(d a) c -> d (a c)", d=128)
                            T.dma("sp", KTt[:].rearrange("d (j r) t -> d r j t", r=4)[:, r],
                                  src[:, g * 1024:(g + 1) * 1024].rearrange("d (j t) -> d j t", t=128),
                                  reads=[("xgk_out", bi)], writes=["KT" + nm])
                            v0 = r * 1024
                            T.dma("sp", Vt[:].rearrange("p (j r) d -> p r j d", r=4)[:, r, :, 0:128],
                                  xgv_out[bi][v0:v0 + 1024, g * 128:(g + 1) * 128].rearrange("(j t) c -> t j c", t=128),
                                  reads=[("xgv_out", bi)], writes=["V" + nm])
                    for j in range(8):
                        Q = qT[:, g, j, :]
                        qk = [("qT", g, j)]
                        T.dma("sp", cmpm[:], c_cmpmask[j], writes=["cmpm"])
                        T.dma("sp", vm[:], c_vm[j], writes=["vm"])
                        T.dma("sp", fb[:], c_fb[j], writes=["fb"])
                        for tl in range(2):
                            sb_ = s_bank()
                            T.op("pe", lambda e, tl=tl, sb_=sb_: e.matmul(
                                ps[sb_][:], lhsT=cKT[:, g, tl * 128:(tl + 1) * 128], rhs=Q, start=True, stop=False),
                                reads=["cKT"] + qk, writes=[("ps", sb_)])
                            T.op("pe", lambda e, tl=tl, sb_=sb_: e.matmul(
                                ps[sb_][:].rearrange("p (a b) -> p a b", a=4), lhsT=identb[:], rhs=cmpm[:, tl, :].unsqueeze(1).broadcast_to([128, 4, 128]), start=False, stop=True),
                                reads=["identb", "cmpm"], writes=[("ps", sb_)])
                            T.op("act", lambda e, tl=tl, sb_=sb_: e.activation(
                                out=P32[tl][:], in_=ps[sb_][:], func=AF.Exp, scale=SCALE),
                                reads=[("ps", sb_)], writes=[("P32", tl)])
                        pbis = []
                        for tl in range(2):
                            pbi = p_buf()
                            pbis.append(pbi)
                            T.op("dve", lambda e, tl=tl, pbi=pbi: e.tensor_copy(out=Pb[pbi][:], in_=P32[tl][:]),
                                 reads=[("P32", tl)], writes=[("Pb", pbi)])
                        for r in range(4):
                            for tl in range(2):
                                T.op("pe", lambda e, r=r, tl=tl: e.matmul(
                                    ps[5][:, r * 65:(r + 1) * 65], lhsT=P32[tl][:, r * 128:(r + 1) * 128],
                                    rhs=ov[:, tl, :], start=(tl == 0), stop=(tl == 1)),
                                    reads=[("P32", tl), "ov"], writes=[("ps", 5)])
                        for tl in range(2):
                            pv_accum(pbis[tl], cV, ["cV"], cV[:, tl, g, :], tl == 0, tl == 1)
                        merge(0, j, g)
                        if 7 > KLEVEL:
                            T.barrier()
                            return
                        pI = ps[5][:, 0:260].rearrange("p (a b) -> p a b", a=4)
                        T.op("dve", lambda e: e.tensor_scalar(out=rsx[:], in0=pI[:, :, 64], scalar1=1e-30,
                                                               scalar2=None, op0=ALU.max),
                             reads=[("ps", 5)], writes=["rsx"])
                        T.op("dve", lambda e: e.reciprocal(out=rsx[:], in_=rsx[:]), reads=["rsx"], writes=["rsx"])
                        T.op("dve", lambda e: e.tensor_scalar(out=imp[:], in0=pI[:, 0, 0:64], scalar1=rsx[:, 0:1],
                                                               scalar2=None, op0=ALU.mult),
                             reads=[("ps", 5), "rsx"], writes=["imp"])
                        for r in range(1, 4):
                            T.op("dve", lambda e, r=r: e.scalar_tensor_tensor(
                                out=imp[:], in0=pI[:, r, 0:64], scalar=rsx[:, r:r + 1], in1=imp[:],
                                op0=ALU.mult, op1=ALU.add), reads=[("ps", 5), "rsx", "imp"], writes=["imp"])
                        T.op("dve", lambda e: e.tensor_tensor(out=score[:], in0=imp[:], in1=vm[:], op=ALU.mult),
                             reads=["imp", "vm"], writes=["score"])
                        T.op("dve", lambda e: e.tensor_tensor(out=score[:], in0=score[:], in1=fb[:], op=ALU.add),
                             reads=["score", "fb"], writes=["score"])
                        T.op("dve", lambda e: e.max(out=m8[:, 0:8], in_=score[:]), reads=["score"], writes=["m8"])
                        T.op("dve", lambda e: e.match_replace(out=score2[:], in_to_replace=m8[:, 0:8],
                                                               in_values=score[:], imm_value=-3e38),
                             reads=["score", "m8"], writes=["score2"])
                        T.op("dve", lambda e: e.max(out=m8[:, 8:16], in_=score2[:]), reads=["score2", "m8"],
                             writes=["m8"])
                        T.op("dve", lambda e: e.tensor_scalar(out=t1[:], in0=score[:], scalar1=m8[:, 15:16],
                                                               scalar2=None, op0=ALU.is_ge),
                             reads=["score", "m8"], writes=["t1"])
                        T.op("dve", lambda e: e.tensor_scalar(out=t2[:], in0=score[:], scalar1=-5e29,
                                                               scalar2=None, op0=ALU.is_gt),
                             reads=["score"], writes=["t2"])
                        T.op("dve", lambda e: e.tensor_tensor(out=t1[:], in0=t1[:], in1=t2[:], op=ALU.mult),
                             reads=["t1", "t2"], writes=["t1"])
                        T.op("dve", lambda e: e.tensor_scalar(out=selb[:], in0=t1[:], scalar1=-1.0, scalar2=BIG,
                                                               op0=ALU.add, op1=ALU.mult),
                             reads=["t1"], writes=["selb"])
                        T.op("dve", lambda e: e.tensor_copy(out=selbx[:],
                                                             in_=selb[:].unsqueeze(2).broadcast_to([128, 64, 64])),
                             reads=["selb"], writes=["selbx"])
                        sx = selbx[:].rearrange("p a b -> p (a b)")
                        if 8 > KLEVEL:
                            T.barrier()
                            return
                        nkt = 4 * j + 4
                        for kt in range(nkt):
                            sb_ = s_bank()
                            dg = kt >= 4 * j
                            T.op("pe", lambda e, kt=kt, sb_=sb_: e.matmul(
                                ps[sb_][:], lhsT=KTs[:, kt, :], rhs=Q, start=True, stop=False),
                                reads=["KTs"] + qk, writes=[("ps", sb_)])
                            T.op("pe", lambda e, kt=kt, sb_=sb_, dg=dg: e.matmul(
                                ps[sb_][:].rearrange("p (a b) -> p a b", a=4), lhsT=sx[:, kt * 128:(kt + 1) * 128], rhs=identb[:].unsqueeze(1).broadcast_to([128, 4, 128]), start=False, stop=not dg),
                                reads=["selbx", "identb"], writes=[("ps", sb_)])
                            if dg:
                                T.op("pe", lambda e, kt=kt, sb_=sb_: e.matmul(
                                    ps[sb_][:].rearrange("p (a b) -> p a b", a=4), lhsT=identb[:], rhs=diagm[:, kt - 4 * j, :].unsqueeze(1).broadcast_to([128, 4, 128]), start=False, stop=True),
                                    reads=["identb", "diagm"], writes=[("ps", sb_)])
                            pbi = p_buf()
                            T.op("act", lambda e, sb_=sb_, pbi=pbi: e.activation(
                                out=Pb[pbi][:], in_=ps[sb_][:], func=AF.Exp, scale=SCALE),
                                reads=[("ps", sb_)], writes=[("Pb", pbi)])
                            pv_accum(pbi, Vs, ["Vs"], Vs[:, kt, :], kt == 0, kt == nkt - 1)
                        merge(1, j, g)
                        if 9 > KLEVEL:
                            T.barrier()
                            return
                        kts = [(m, 4 * j - 4 + m) for m in range(8) if 4 * j - 4 + m >= 0]
                        for ii, (m, kt) in enumerate(kts):
                            sb_ = s_bank()
                            T.op("pe", lambda e, kt=kt, sb_=sb_: e.matmul(
                                ps[sb_][:], lhsT=KTw[:, kt, :], rhs=Q, start=True, stop=False),
                                reads=["KTw"] + qk, writes=[("ps", sb_)])
                            T.op("pe", lambda e, m=m, sb_=sb_: e.matmul(
                                ps[sb_][:].rearrange("p (a b) -> p a b", a=4), lhsT=identb[:], rhs=winm[:, m, :].unsqueeze(1).broadcast_to([128, 4, 128]), start=False, stop=True),
                                reads=["identb", "winm"], writes=[("ps", sb_)])
                            pbi = p_buf()
                            T.op("act", lambda e, sb_=sb_, pbi=pbi: e.activation(
                                out=Pb[pbi][:], in_=ps[sb_][:], func=AF.Exp, scale=SCALE),
                                reads=[("ps", sb_)], writes=[("Pb", pbi)])
                            pv_accum(pbi, Vw, ["Vw"], Vw[:, kt, :], ii == 0, ii == len(kts) - 1)
                        merge(2, j, g)
                        T.op("act", lambda e: e.copy(out=mixb[:], in_=mix[:]),
                             reads=[("mix", h) for h in range(4)], writes=["mixb"])
                        for h in range(4):
                            T.op("pe", lambda e, h=h: e.transpose(out=psb[:, h * 128:(h + 1) * 128], in_=mixb[:, h, :],
                                                                  identity=identb[:]),
                                 reads=["mixb", "identb"], writes=["psb"])
                        T.op("act", lambda e, j=j: e.copy(out=qT[:, g, j, :], in_=psb[:, 0:512]),
                             reads=["psb"], writes=[("qT", g, j)])
                    if not enable_sample:
                        T.op("dve", lambda e, g=g: e.memset(qs[:, g, :], 0.0), writes=[("qT", g, 8)])
                T.barrier()

            if enable_sample:
                sample_attn(l, qs, gates, pep, w1b, w2b, PM, PMK, ktnew, vnew)
                T.barrier()

            def mrhs(k, t0, tn):
                g, h = k // 4, k % 4
                if t0 < NTP:
                    j0 = t0 // 128
                    return qT[:, g, j0:j0 + 4, h * 128:(h + 1) * 128]
                return qs[:, g, h * 8:(h + 1) * 8]
            allq = [("qT", g, j) for g in range(4) for j in range(9)]
            out_proj(attn_w_out[l], mrhs, allq)
            T.barrier()

    def final_out():
        with ExitStack() as es:
            yT = tmp(es, "yT", [128, KC, NT])
            ost = [tmp(es, "ost", [128, D]) for _ in range(2)]
            rmsnorm(8, yT, okey="yT")
            cnt = 0
            for t in range(9):
                n = 128 if t < 8 else NS
                st = ost[t % 2]
                for q in range(4):
                    bi = cnt % 4
                    cnt += 1
                    for k in range(4):
                        kc = q * 4 + k
                        T.op("pe", lambda e, kc=kc, k=k, bi=bi, n=n, t=t: e.transpose(
                            out=ps[bi][0:n, k * 128:(k + 1) * 128], in_=yT[:, kc, t * 128:t * 128 + n],
                            identity=identf[:]),
                            reads=[("yT", kc), "identf"], writes=[("ps", bi)])
                    T.op("act" if q % 2 else "dve",
                         (lambda e, q=q, bi=bi, n=n, st=st: e.copy(out=st[0:n, q * 512:(q + 1) * 512], in_=ps[bi][0:n, :]))
                         if q % 2 else
                         (lambda e, q=q, bi=bi, n=n, st=st: e.tensor_copy(out=st[0:n, q * 512:(q + 1) * 512],
                                                                         in_=ps[bi][0:n, :])),
                         reads=[("ps", bi)], writes=[("ost", t % 2, q)])
                dst = yp[t * 128:(t + 1) * 128, :] if t < 8 else ys[:, :]
                T.dma("sp", dst, st[0:n, :], reads=[("ost", t % 2, q) for q in range(4)])

    import os
    dbg = os.environ.get("KDEBUG", "")
    load_x()
    for i in range(4):
        if dbg and ("L%d" % i) not in dbg:
            continue
        if i % 2 == 0:
            if "noattn" not in dbg:
                attn_layer(i // 2)
        else:
            if "noconv" not in dbg:
                conv_layer(i // 2)
        if "noffn" not in dbg:
            ffn(i)
    final_out()
    T.finish()
    return nc


def _bf(a):
    return np.asarray(a, dtype=np.float32).astype(ml_dtypes.bfloat16)


def _core_consts(c):
    ci = c % 4
    k = np.arange(128)[:, None]
    tt = np.arange(128)[None, :]
    d = {}
    d["c_identf"] = np.eye(128, dtype=np.float32)
    d["c_identb"] = _bf(np.eye(128))
    d["c_i4"] = _bf(np.tile(np.eye(128, dtype=np.float32), (1, 4)))
    pm = np.zeros((128, 16), np.float32)
    for hf in range(2):
        for cch in range(8):
            pm[cch * 16:(cch + 1) * 16, hf * 8 + cch] = 1.0
    d["c_pmmask"] = pm
    d["c_oh16"] = (np.arange(128)[:, None] % 16 == np.arange(16)[None, :]).astype(np.float32)
    cm = np.zeros((8, 128, 2, 128), np.float32)
    for j in range(8):
        t = 128 * (4 * j + ci) + np.arange(128)
        for tl in range(2):
            i = tl * 128 + np.arange(128)
            vis = (i[:, None] <= 254) & (16 * i[:, None] + 31 <= t[None, :])
            cm[j, :, tl, :] = np.where(vis, 0.0, -BIG)
    d["c_cmpmask"] = _bf(cm)
    dm = np.zeros((128, 4, 128), np.float32)
    for m in range(4):
        if m < ci:
            v = np.zeros((128, 128), np.float32)
        elif m == ci:
            v = np.where(k <= tt, 0.0, -BIG)
        else:
            v = np.full((128, 128), -BIG, np.float32)
        dm[:, m, :] = v
    d["c_diagmask"] = _bf(dm)
    wm = np.zeros((128, 8, 128), np.float32)
    for m in range(8):
        dd = m - 4 - ci
        if dd == -4:
            v = np.where(k > tt, 0.0, -BIG)
        elif dd in (-3, -2, -1):
            v = np.zeros((128, 128), np.float32)
        elif dd == 0:
            v = np.where(k <= tt, 0.0, -BIG)
        else:
            v = np.full((128, 128), -BIG, np.float32)
        wm[:, m, :] = v
    d["c_winmask"] = _bf(wm)
    vm = np.zeros((8, 128, 64), np.float32)
    fb = np.zeros((8, 128, 64), np.float32)
    jb = np.arange(64)[None, :]
    for j in range(8):
        t = (128 * (4 * j + ci) + np.arange(128))[:, None]
        cur = t // 64
        invalid = jb > cur
        f0 = (jb == 0)
        f1 = (jb == cur)
        f2 = (jb == cur - 1)
        forced = f0 | f1 | f2
        vm[j] = np.where(forced | invalid, 0.0, 1.0)
        b = np.zeros((128, 64), np.float32)
        b = np.where(invalid, -1e30, b)
        b = np.where(f2, 3e30, b)
        b = np.where(f1, 2e30, b)
        b = np.where(f0, 1e30, b)
        fb[j] = b
    d["c_vm"] = vm
    d["c_fb"] = fb
    ov = np.zeros((128, 2, 65), np.float32)
    for tl in range(2):
        for cc in range(128):
            i = tl * 128 + cc
            if i > 254:
                continue
            for jbb in range(64):
                if 4 * jbb - 1 <= i <= 4 * jbb + 3:
                    ov[cc, tl, jbb] = 1.0
            ov[cc, tl, 64] = 1.0
    d["c_ov"] = ov
    hs = np.zeros((128, 5), np.float32)
    if ci >= 1:
        hs[:, ci - 1] = 1.0
    else:
        hs[:, 4] = 1.0
    d["c_hsel"] = hs
    return d


_NC_CACHE = {}


def _sample_consts():
    d = {}
    d["c_piota"] = np.arange(128, dtype=np.float32)[:, None]
    ovs = np.zeros((128, 34), np.float32)
    for cl in range(128):
        for jj in range(33):
            if 4 * jj - 1 <= cl <= 4 * jj + 3:
                ovs[cl, jj] = 1.0
        ovs[cl, 33] = 1.0
    d["c_ovs"] = ovs
    vms = np.ones((8, 257), np.float32)
    fbs = np.zeros((8, 257), np.float32)
    for jb, val in ((0, 1e30), (256, 2e30), (255, 3e30)):
        vms[:, jb] = 0.0
        fbs[:, jb] = val
    d["c_vms"] = vms
    d["c_fbs"] = fbs
    d8 = np.zeros((8, 4, 8), np.float32)
    tri8 = np.zeros((8, 4, 8), np.float32)
    wins0 = np.zeros((128, 4, 8), np.float32)
    for t in range(8):
        d8[t, :, t] = 1.0
        tri8[t + 1:, :, t] = -BIG
        wins0[:t + 1, :, t] = -BIG
    d["c_d8"] = _bf(d8.reshape(8, 32))
    d["c_tri8"] = _bf(tri8.reshape(8, 32))
    d["c_wins0"] = _bf(wins0.reshape(128, 32))
    return d


def kernel(**inp):
    import os
    enable_sample = os.environ.get("KNOSAMPLE", "") == ""
    if enable_sample not in _NC_CACHE:
        _NC_CACHE[enable_sample] = build(enable_sample)
    nc = _NC_CACHE[enable_sample]
    f = lambda a: np.ascontiguousarray(np.asarray(a, dtype=np.float32))
    x_prompt = f(inp["x_prompt"])
    x_sample = f(inp["x_sample"])
    gains = np.concatenate([f(inp["attn_norm"]), f(inp["conv_norm"]), f(inp["ffn_norm"]),
                            f(inp["final_norm"])[None, :]], axis=0)
    shared = {
        "gains": gains,
        "attn_w_in": f(inp["attn_w_in"]), "attn_w_out": f(inp["attn_w_out"]),
        "attn_cmp_pool": f(inp["attn_cmp_pool"]), "attn_cmp_pe": f(inp["attn_cmp_pe"]),
        "attn_cmp_w1": f(inp["attn_cmp_w1"]), "attn_cmp_w2": f(inp["attn_cmp_w2"]),
        "conv_w_in": f(inp["conv_w_in"]), "conv_kernel": f(inp["conv_kernel"]),
        "conv_w_out": f(inp["conv_w_out"]), "ffn_w_in": f(inp["ffn_w_in"]), "ffn_w_out": f(inp["ffn_w_out"]),
    }
    state_conv = f(inp["state_conv"])
    if enable_sample:
        cc_all = f(inp["cache_cmp_kv"]).reshape(2 * 1280 * 128, 1024)
        cs_all = f(inp["cache_sel_kv"]).reshape(2 * 1280 * 128, 1024)
        sconst = _sample_consts()
    in_maps = []
    for c in range(8):
        b, ci = c // 4, c % 4
        m = dict(shared)
        m["xp"] = np.ascontiguousarray(x_prompt[b].reshape(32, 128, D)[ci::4].reshape(NTP, D))
        m["xs"] = np.ascontiguousarray(x_sample[c])
        m["state_conv"] = np.ascontiguousarray(state_conv[:, c])
        m["state_win"] = np.ascontiguousarray(f(inp["state_win_kv"])[:, c].reshape(2, 512, 1024))
        m.update(_core_consts(c))
        if enable_sample:
            m["cache_cmp_kv"] = cc_all
            m["cache_sel_kv"] = cs_all
            m["page_tab"] = np.ascontiguousarray(np.asarray(inp["page_table"], dtype=np.int32)[c:c + 1])
            m.update(sconst)
        in_maps.append(m)
    res = run_bass_kernel_spmd(nc, in_maps, core_ids=list(range(8)))
    R = res.results
    y_prompt = np.zeros((2, 4096, D), np.float32)
    y_sample = np.zeros((8, NS, D), np.float32)
    ncp = np.zeros((2, 2, 4096, 4, 2, 128), np.float32)
    nsp = np.zeros((2, 2, 4096, 4, 2, 128), np.float32)
    nwp = np.zeros((2, 2, 512, 4, 2, 128), np.float32)
    ncvp = np.zeros((2, 2, 2, D), np.float32)
    ncs = np.zeros((2, 8, NS, 4, 2, 128), np.float32)
    nss = np.zeros((2, 8, NS, 4, 2, 128), np.float32)
    nws = np.zeros((2, 8, 512, 4, 2, 128), np.float32)
    ncvs = np.zeros((2, 8, 2, D), np.float32)
    for c in range(8):
        b, ci = c // 4, c % 4
        r = R[c]
        y_prompt[b].reshape(32, 128, D)[ci::4] = r["yp"].reshape(8, 128, D)
        y_sample[c] = r["ys"]
        kvp = r["kvp"].reshape(2, 3, 8, 128, 4, 2, 128)
        for l in range(2):
            ncp[l, b].reshape(32, 128, 4, 2, 128)[ci::4] = kvp[l, 0]
            nsp[l, b].reshape(32, 128, 4, 2, 128)[ci::4] = kvp[l, 1]
            nwp[l, b].reshape(4, 128, 4, 2, 128)[ci] = kvp[l, 2, 7]
            if ci == 3:
                ncvp[l, b] = r["convp"][l]
            kvs_ = r["kvs"].reshape(2, 3, NS, 4, 2, 128)
            ncs[l, c] = kvs_[l, 0]
            nss[l, c] = kvs_[l, 1]
            nws[l, c] = r["wins"][l].reshape(512, 4, 2, 128)
            ncvs[l, c] = r["convs"][l]
    return (y_prompt, y_sample, ncp, nsp, nwp, ncvp, ncs, nss, nws, ncvs)
```
